# Optimizing a Trainium2 kernel written in Bass

```python
import math
import jax, jax.numpy as jnp
from jax import lax
import numpy as np

D_MODEL = 1024
BATCH = 16
SEQ = 256
DEPTH = 4
DEC_BATCH = 2
DEC_SEQ = 1024
PAST_LEN = 256

GRID_W = 64
ATT_HEADS = 8
ATT_KV_HEADS = 2
ATT_GROUP = ATT_HEADS // ATT_KV_HEADS
HEAD_DIM = 64
WINDOW = 128
ATT_BLOCK = 128
ROPE_BASE = 10000.0
GLA_HEADS = 4
GLA_DK = 64
GLA_DV = 128
GLA_RANK = 16
GLA_TAU = 16.0
GLA_CHUNK = 64
POOL_GROUPS = 4
POOL_GROUP_DIM = 128
POOL_WINDOWS = (2, 4, 8, 16)
D_FF = -(-8 * D_MODEL // (3 * 256)) * 256
ATT_Q = ATT_HEADS * HEAD_DIM
ATT_KV = ATT_KV_HEADS * HEAD_DIM
GLA_QK = GLA_HEADS * GLA_DK
GLA_VW = GLA_HEADS * GLA_DV
POOL_W = POOL_GROUPS * POOL_GROUP_DIM
IN_SPLITS = (ATT_Q, ATT_KV, ATT_KV, GLA_QK, GLA_QK, GLA_VW, GLA_VW, GLA_RANK, GLA_RANK, POOL_W, D_MODEL, D_MODEL, D_MODEL)
IN_WIDTH = sum(IN_SPLITS)
EPS = 1e-6
NEG = -1e30

kernel_name = 'hybrid_dit_prefix_ctx_step'


def _rmsnorm(x, g):
    xf = x.astype(jnp.float32)
    y = xf * lax.rsqrt(jnp.mean(xf * xf, axis=-1, keepdims=True) + EPS)
    return (y * g.astype(jnp.float32)).astype(x.dtype)


def _rope_1d(x, pos):
    half = x.shape[-1] // 2
    inv_freq = ROPE_BASE ** (-jnp.arange(half, dtype=jnp.float32) / half)
    ang = pos[:, None] * inv_freq[None, :]
    cos = jnp.cos(ang)[None, :, None, :]
    sin = jnp.sin(ang)[None, :, None, :]
    xf = x.astype(jnp.float32)
    x1, x2 = xf[..., :half], xf[..., half:]
    return jnp.concatenate([x1 * cos - x2 * sin, x2 * cos + x1 * sin], axis=-1).astype(x.dtype)


def _rope_2d(x):
    T = x.shape[1]
    rows = T // GRID_W
    row = jnp.repeat(jnp.arange(rows, dtype=jnp.float32), GRID_W)
    col = jnp.tile(jnp.arange(GRID_W, dtype=jnp.float32), rows)
    h = HEAD_DIM // 2
    return jnp.concatenate([_rope_1d(x[..., :h], row), _rope_1d(x[..., h:], col)], axis=-1)


def _sink_softmax_av(s, v, sink):
    sk = sink.astype(jnp.float32)[None, :, :, None, None]
    m = jnp.maximum(jnp.max(s, axis=-1, keepdims=True), sk)
    p = jnp.exp(s - m)
    p = p / (jnp.sum(p, axis=-1, keepdims=True) + jnp.exp(sk - m))
    return jnp.einsum('bkgqs,bskd->bqkgd', p, v.astype(jnp.float32))


def _ctx_attention(q, k, v, sink):
    B, T = q.shape[0], q.shape[1]
    nb = T // ATT_BLOCK
    scale = HEAD_DIM ** -0.5
    qb = q.astype(jnp.float32).reshape(B, nb, ATT_BLOCK, ATT_KV_HEADS, ATT_GROUP, HEAD_DIM).swapaxes(0, 1)
    kf = k.astype(jnp.float32)

    def one(qblk):
        s = jnp.einsum('bqkgd,bskd->bkgqs', qblk, kf) * scale
        return _sink_softmax_av(s, v, sink)

    o = lax.map(one, qb)
    return o.swapaxes(0, 1).reshape(B, T, ATT_Q).astype(q.dtype)


def _latent_attention(q, k, v, kc, vc, sink):
    B, T = q.shape[0], q.shape[1]
    nb = T // ATT_BLOCK
    scale = HEAD_DIM ** -0.5
    pad = ((0, 0), (ATT_BLOCK, ATT_BLOCK), (0, 0), (0, 0))
    kp = jnp.pad(k.astype(jnp.float32), pad)
    vp = jnp.pad(v.astype(jnp.float32), pad)
    qf = q.astype(jnp.float32).reshape(B, T, ATT_KV_HEADS, ATT_GROUP, HEAD_DIM)
    kcf = kc.astype(jnp.float32)
    vcf = vc.astype(jnp.float32)
    a = jnp.arange(ATT_BLOCK, dtype=jnp.int32)[:, None]
    j = jnp.arange(3 * ATT_BLOCK, dtype=jnp.int32)[None, :]
    band = jnp.abs(j - ATT_BLOCK - a) <= WINDOW

    def one(i):
        start = i * ATT_BLOCK
        qblk = lax.dynamic_slice_in_dim(qf, start, ATT_BLOCK, axis=1)
        kblk = lax.dynamic_slice_in_dim(kp, start, 3 * ATT_BLOCK, axis=1)
        vblk = lax.dynamic_slice_in_dim(vp, start, 3 * ATT_BLOCK, axis=1)
        kpos = start - ATT_BLOCK + j
        valid = band & (kpos >= 0) & (kpos < T)
        s_loc = jnp.einsum('bqkgd,bskd->bkgqs', qblk, kblk) * scale
        s_loc = jnp.where(valid, s_loc, NEG)
        s_ctx = jnp.einsum('bqkgd,bskd->bkgqs', qblk, kcf) * scale
        s = jnp.concatenate([s_loc, s_ctx], axis=-1)
        vall = jnp.concatenate([vblk, vcf], axis=1)
        return _sink_softmax_av(s, vall, sink)

    o = lax.map(one, jnp.arange(nb, dtype=jnp.int32))
    return o.swapaxes(0, 1).reshape(B, T, ATT_Q).astype(q.dtype)


def _gla_chunked(q, k, v, log_a, s0):
    B, T = q.shape[0], q.shape[1]
    n = T // GLA_CHUNK

    def chunks(t):
        return t.reshape(B, n, GLA_CHUNK, GLA_HEADS, t.shape[-1]).transpose(1, 0, 3, 2, 4)

    qc, kc, vc = chunks(q), chunks(k), chunks(v)
    bc = jnp.cumsum(chunks(log_a), axis=3)
    causal = jnp.tril(jnp.ones((GLA_CHUNK, GLA_CHUNK), dtype=bool))[:, :, None]

    def step(S, inp):
        q_, k_, v_, b_ = inp
        inter = jnp.einsum('bhtd,bhde->bhte', q_ * jnp.exp(b_), S)
        diff = b_[:, :, :, None, :] - b_[:, :, None, :, :]
        decay = jnp.exp(jnp.where(causal, diff, -jnp.inf))
        A = jnp.einsum('bhtd,bhsd,bhtsd->bhts', q_, k_, decay)
        intra = jnp.einsum('bhts,bhse->bhte', A, v_)
        b_last = b_[:, :, -1:, :]
        S_new = jnp.exp(b_[:, :, -1, :])[..., None] * S + jnp.einsum('bhsd,bhse->bhde', k_ * jnp.exp(b_last - b_), v_)
        return S_new, inter + intra

    S_fin, o = lax.scan(step, s0, (qc, kc, vc, bc))
    o = o.transpose(1, 0, 3, 2, 4).reshape(B, T, GLA_HEADS, GLA_DV)
    return o, S_fin


def _gla_bidir(q, k, v, la_f, la_b, s0_f, s0_b):
    o_f, S_f = _gla_chunked(q, k, v, la_f, s0_f)
    fl = lambda t: jnp.flip(t, axis=1)
    o_b, S_b = _gla_chunked(fl(q), fl(k), fl(v), fl(la_b), s0_b)
    return o_f + fl(o_b), S_f, S_b


def _pool_mix(u, w_pool, scale):
    B, T = u.shape[0], u.shape[1]
    uf = u.astype(jnp.float32).reshape(B, T, POOL_GROUPS, POOL_GROUP_DIM)
    cs = jnp.concatenate([jnp.zeros_like(uf[:, :1]), jnp.cumsum(uf, axis=1)], axis=1)
    w = jnp.array(POOL_WINDOWS, dtype=jnp.int32)[None, :]
    t = jnp.arange(T, dtype=jnp.int32)[:, None]
    lo = jnp.clip(t - w // 2, 0, T)
    hi = jnp.clip(t - w // 2 + w, 0, T)
    gi = jnp.arange(POOL_GROUPS, dtype=jnp.int32)[None, :]
    win_sum = cs[:, hi, gi] - cs[:, lo, gi]
    cnt = (hi - lo).astype(jnp.float32)[None, :, :, None]
    pooled = win_sum / cnt - uf
    y = jnp.einsum('btgi,gio->btgo', pooled, w_pool.astype(jnp.float32)).reshape(B, T, POOL_W)
    return (y * scale.astype(jnp.float32)).astype(u.dtype)


def _mixer(hn, p, ctx):
    B, T = hn.shape[0], hn.shape[1]
    z = hn @ p['w_in']
    qa, ka, va, qb, kb, vb, rb, glf, glb, uc, ga, gb, gc = jnp.split(z, [int(s) for s in np.cumsum(IN_SPLITS)[:-1]], axis=-1)
    qa = _rmsnorm(qa.reshape(B, T, ATT_HEADS, HEAD_DIM), p['g_qn'])
    ka = _rmsnorm(ka.reshape(B, T, ATT_KV_HEADS, HEAD_DIM), p['g_kn'])
    va = va.reshape(B, T, ATT_KV_HEADS, HEAD_DIM)
    sink = p['att_sink'].reshape(ATT_KV_HEADS, ATT_GROUP)
    qb = qb.astype(jnp.float32).reshape(B, T, GLA_HEADS, GLA_DK) * (GLA_DK ** -0.5)
    kb = kb.astype(jnp.float32).reshape(B, T, GLA_HEADS, GLA_DK)
    vb = vb.astype(jnp.float32).reshape(B, T, GLA_HEADS, GLA_DV)
    la_f = (jax.nn.log_sigmoid((glf @ p['w_gate2'][0] + p['b_gate2'][0]).astype(jnp.float32)) / GLA_TAU).reshape(B, T, GLA_HEADS, GLA_DK)
    la_b = (jax.nn.log_sigmoid((glb @ p['w_gate2'][1] + p['b_gate2'][1]).astype(jnp.float32)) / GLA_TAU).reshape(B, T, GLA_HEADS, GLA_DK)
    if ctx is None:
        o_a = _ctx_attention(qa, ka, va, sink)
        s0 = jnp.zeros((B, GLA_HEADS, GLA_DK, GLA_DV), jnp.float32)
        o_b, S_f, S_b = _gla_bidir(qb, kb, vb, la_f, la_b, s0, s0)
        new = (ka, va, jnp.stack([S_f, S_b], axis=1))
    else:
        kc, vc, st = ctx
        o_a = _latent_attention(_rope_2d(qa), _rope_2d(ka), va, kc, vc, sink)
        o_b, _, _ = _gla_bidir(qb, kb, vb, la_f, la_b, st[:, 0].astype(jnp.float32), st[:, 1].astype(jnp.float32))
        new = None
    o_b = _rmsnorm(o_b, p['g_gla_out']).reshape(B, T, GLA_VW).astype(hn.dtype) * jax.nn.silu(rb)
    o_c = _pool_mix(uc, p['w_pool'], p['pool_scale'])
    mixed = (jax.nn.sigmoid(ga) * (o_a @ p['w_br_a'])
             + jax.nn.sigmoid(gb) * (o_b @ p['w_br_b'])
             + jax.nn.sigmoid(gc) * (o_c @ p['w_br_c']))
    return mixed @ p['w_out'], new


def _layer(x, cvec, p, ctx):
    mod = (jax.nn.silu(cvec) @ p['w_mod'] + p['b_mod']).reshape(-1, 1, 6 * D_MODEL)
    sh1, sc1, g1, sh2, sc2, g2 = jnp.split(mod, 6, axis=-1)
    hn = _rmsnorm(x, p['g_norm1']) * (1 + sc1) + sh1
    mix, new = _mixer(hn, p, ctx)
    x = x + g1 * mix
    hn = _rmsnorm(x, p['g_norm2']) * (1 + sc2) + sh2
    x = x + g2 * ((jax.nn.silu(hn @ p['w_ff_gate']) * (hn @ p['w_ff_up'])) @ p['w_ff_down'])
    return x, new


def setup_inputs(seed: int = 0) -> dict:
    key = jax.random.key(seed)
    ks = jax.random.split(key, 32)
    f32 = jnp.float32
    nrm = lambda k, shape, s: jax.random.normal(k, shape, f32) * s
    gain = lambda k, shape: 1.0 + 0.02 * jax.random.normal(k, shape, f32)
    D = D_MODEL
    return {
        'x_prompt': nrm(ks[0], (BATCH, SEQ, D), 1.0),
        'x_sample': nrm(ks[1], (DEC_BATCH, DEC_SEQ, D), 1.0),
        'c': nrm(ks[2], (DEC_BATCH, D), 1.0),
        'cache_k': nrm(ks[3], (DEC_BATCH, DEPTH, PAST_LEN, ATT_KV_HEADS, HEAD_DIM), 1.0),
        'cache_v': nrm(ks[4], (DEC_BATCH, DEPTH, PAST_LEN, ATT_KV_HEADS, HEAD_DIM), 1.0),
        'state_gla': nrm(ks[5], (DEC_BATCH, DEPTH, 2, GLA_HEADS, GLA_DK, GLA_DV), 1.0),
        'c_ctx': nrm(ks[6], (D,), 1.0),
        'w_in': nrm(ks[7], (DEPTH, D, IN_WIDTH), D ** -0.5),
        'g_qn': gain(ks[8], (DEPTH, HEAD_DIM)),
        'g_kn': gain(ks[9], (DEPTH, HEAD_DIM)),
        'att_sink': nrm(ks[10], (DEPTH, ATT_HEADS), 0.5),
        'w_gate2': nrm(ks[11], (DEPTH, 2, GLA_RANK, GLA_QK), GLA_RANK ** -0.5),
        'b_gate2': nrm(ks[12], (DEPTH, 2, GLA_QK), 0.1),
        'g_gla_out': gain(ks[13], (DEPTH, GLA_DV)),
        'w_pool': nrm(ks[14], (DEPTH, POOL_GROUPS, POOL_GROUP_DIM, POOL_GROUP_DIM), POOL_GROUP_DIM ** -0.5),
        'pool_scale': 1.0 + 0.1 * jax.random.normal(ks[15], (DEPTH, POOL_W), f32),
        'w_br_a': nrm(ks[16], (DEPTH, ATT_Q, D), ATT_Q ** -0.5),
        'w_br_b': nrm(ks[17], (DEPTH, GLA_VW, D), GLA_VW ** -0.5),
        'w_br_c': nrm(ks[18], (DEPTH, POOL_W, D), POOL_W ** -0.5),
        'w_out': nrm(ks[19], (DEPTH, D, D), D ** -0.5),
        'g_norm1': gain(ks[20], (DEPTH, D)),
        'g_norm2': gain(ks[21], (DEPTH, D)),
        'w_mod': nrm(ks[22], (DEPTH, D, 6 * D), D ** -0.5),
        'b_mod': nrm(ks[23], (DEPTH, 6 * D), 0.02),
        'w_ff_gate': nrm(ks[24], (DEPTH, D, D_FF), D ** -0.5),
        'w_ff_up': nrm(ks[25], (DEPTH, D, D_FF), D ** -0.5),
        'w_ff_down': nrm(ks[26], (DEPTH, D_FF, D), D_FF ** -0.5),
    }


def reference(x_prompt, x_sample, c, cache_k, cache_v, state_gla, c_ctx, w_in, g_qn, g_kn, att_sink,
              w_gate2, b_gate2, g_gla_out, w_pool, pool_scale, w_br_a, w_br_b, w_br_c, w_out,
              g_norm1, g_norm2, w_mod, b_mod, w_ff_gate, w_ff_up, w_ff_down):
    yp = x_prompt
    ys = x_sample
    new_ks, new_vs, new_sts = [], [], []
    for l in range(DEPTH):
        p = {
            'w_in': w_in[l], 'g_qn': g_qn[l], 'g_kn': g_kn[l], 'att_sink': att_sink[l],
            'w_gate2': w_gate2[l], 'b_gate2': b_gate2[l], 'g_gla_out': g_gla_out[l],
            'w_pool': w_pool[l], 'pool_scale': pool_scale[l],
            'w_br_a': w_br_a[l], 'w_br_b': w_br_b[l], 'w_br_c': w_br_c[l], 'w_out': w_out[l],
            'g_norm1': g_norm1[l], 'g_norm2': g_norm2[l], 'w_mod': w_mod[l], 'b_mod': b_mod[l],
            'w_ff_gate': w_ff_gate[l], 'w_ff_up': w_ff_up[l], 'w_ff_down': w_ff_down[l],
        }
        yp, (k_l, v_l, s_l) = _layer(yp, c_ctx, p, None)
        new_ks.append(k_l)
        new_vs.append(v_l)
        new_sts.append(s_l)
        ys, _ = _layer(ys, c, p, (cache_k[:, l], cache_v[:, l], state_gla[:, l]))
    new_k = jnp.stack(new_ks, axis=1)
    new_v = jnp.stack(new_vs, axis=1)
    new_state_gla = jnp.stack(new_sts, axis=1)
    return (yp, ys, new_k, new_v, new_state_gla)
```

```python
import contextlib
import numpy as np
import concourse.bass as bass
import concourse.mybir as mybir
from concourse.bass_utils import run_bass_kernel_spmd

F32 = mybir.dt.float32
BF16 = mybir.dt.bfloat16
AF = mybir.ActivationFunctionType
ALU = mybir.AluOpType

ENGS = ("pe", "act", "dve", "pool", "sp")

T = 1024
NB = 8
D = 1024
L = 4
DFF = 2816
NFT = 22
EPS = 1e-6
W_IN = 5920
NSLOT = 4
NTF = 8
NTB = 6
DBG = {}
SLOTW = 4480


class Op:
    __slots__ = ("eng", "fn", "deps", "signal", "idx", "dma_slot", "dma_val", "waits")

    def __init__(self, eng, fn):
        self.eng = eng
        self.fn = fn
        self.deps = set()
        self.signal = False
        self.idx = 0
        self.dma_slot = None
        self.dma_val = 0
        self.waits = []


class Prog:
    def __init__(self, nc):
        self.nc = nc
        self.ops = []
        self.by_eng = {e: [] for e in ENGS}
        self.last_w = {}
        self.readers = {}
        self.dma_slots = {}
        self.alias_owner = {}
        self.alias_flip = {}
        self.alias_of = {}
        self.keys_of = {}

    def alias(self, group, *names):
        for n in names:
            self.alias_of.setdefault(n, []).append(group)

    def add(self, eng, fn, reads=(), writes=(), dma_slot=None):
        op = Op(eng, fn)
        reads = list(reads)
        writes = list(writes)
        psr = [k for k in reads if isinstance(k, tuple) and k[0] == "ps"]
        if psr:
            reads = [k for k in reads if k not in psr]
            writes = writes + [k for k in psr if k not in writes]
        for k in reads + writes:
            n = k[0] if isinstance(k, tuple) else k
            for g in self.alias_of.get(n, ()):
                own = self.alias_owner.get(g)
                if own != n:
                    fd = set()
                    if own is not None:
                        for ok in self.keys_of.get(own, ()):
                            w = self.last_w.get(ok)
                            if w is not None:
                                fd.add(w)
                            for r in self.readers.get(ok, ()):
                                fd.add(r)
                    self.alias_flip[g] = fd
                    self.alias_owner[g] = n
                op.deps |= self.alias_flip.get(g, set())
            if n in self.alias_of:
                self.keys_of.setdefault(n, set()).add(k)
        for k in reads:
            w = self.last_w.get(k)
            if w is not None:
                op.deps.add(w)
        for k in writes:
            w = self.last_w.get(k)
            if w is not None:
                op.deps.add(w)
            for r in self.readers.get(k, ()):
                op.deps.add(r)
        for k in reads:
            self.readers.setdefault(k, []).append(op)
        for k in writes:
            self.last_w[k] = op
            self.readers[k] = []
        if "*" in reads:
            for e in ENGS:
                if self.by_eng[e]:
                    op.deps.add(self.by_eng[e][-1])
            for o2 in self.ops:
                if o2.dma_slot is not None:
                    op.deps.add(o2)
        op.deps.discard(op)
        if dma_slot is not None:
            op.dma_slot = dma_slot
            c = self.dma_slots.get(dma_slot, 0) + 1
            self.dma_slots[dma_slot] = c
            op.dma_val = 16 * c
        self.ops.append(op)
        self.by_eng[eng].append(op)
        return op

    def emit(self):
        nc = self.nc
        for op in self.ops:
            for d in op.deps:
                if d.dma_slot is not None:
                    continue
                if d.eng == op.eng and op.eng == "pe" and op.dma_slot is None:
                    continue
                d.signal = True
        cnt = {e: 0 for e in ENGS}
        for op in self.ops:
            if op.dma_slot is None and op.signal:
                cnt[op.eng] += 1
                op.idx = cnt[op.eng]
        final_cnt = dict(cnt)
        waited = {e: {} for e in ENGS}
        for op in self.ops:
            need = {}
            for d in op.deps:
                if d.dma_slot is not None:
                    key = ("dma", d.dma_slot)
                    need[key] = max(need.get(key, 0), d.dma_val)
                else:
                    if not d.signal:
                        continue
                    if d.eng == op.eng and op.eng == "pe" and op.dma_slot is None:
                        continue
                    key = ("eng", d.eng)
                    need[key] = max(need.get(key, 0), d.idx)
            w = waited[op.eng]
            for key, v in need.items():
                if w.get(key, 0) >= v:
                    continue
                w[key] = v
                op.waits.append((key, v))
        if DBG.get('sim'):
            sem = {}
            pos = {e: 0 for e in ENGS}
            prog = True
            while prog:
                prog = False
                for e in ENGS:
                    while pos[e] < len(self.by_eng[e]):
                        op = self.by_eng[e][pos[e]]
                        if all(sem.get(k, 0) >= v for k, v in op.waits):
                            if op.dma_slot is not None:
                                sem[('dma', op.dma_slot)] = sem.get(('dma', op.dma_slot), 0) + 16
                            elif op.signal:
                                sem[('eng', e)] = sem.get(('eng', e), 0) + 1
                                assert sem[('eng', e)] == op.idx
                            pos[e] += 1
                            prog = True
                        else:
                            break
            print('SIM done:', {e: (pos[e], len(self.by_eng[e])) for e in ENGS})
            for e in ENGS:
                if pos[e] < len(self.by_eng[e]):
                    op = self.by_eng[e][pos[e]]
                    print('STUCK', e, pos[e], op.waits, {k: sem.get(k, 0) for k, v in op.waits})
        if DBG.get('print'):
            for e in ENGS:
                print('ENGINE', e)
                for op in self.by_eng[e]:
                    print('   ', op.waits, 'sig' if op.signal else '', op.idx, op.dma_slot, op.dma_val)
        with contextlib.ExitStack() as es:
            esem = {e: es.enter_context(nc.semaphore("c_" + e)) for e in ENGS}
            dsem = {s: es.enter_context(nc.semaphore("d_%d" % i)) for i, s in enumerate(self.dma_slots)}
            block = es.enter_context(nc.Block())

            def run(engname, eng):
                for op in self.by_eng[engname]:
                    for key, v in op.waits:
                        if key[0] == "dma":
                            eng.wait_ge(dsem[key[1]], v)
                        else:
                            eng.wait_ge(esem[key[1]], v)
                    ins = op.fn(eng)
                    if op.dma_slot is not None:
                        ins.then_inc(dsem[op.dma_slot], 16)
                    elif op.signal:
                        ins.then_inc(esem[engname], 1)
                if engname == "sp":
                    for s, c in self.dma_slots.items():
                        eng.wait_ge(dsem[s], 16 * c)
                    for e in ENGS:
                        if e != "sp" and final_cnt[e] > 0:
                            eng.wait_ge(esem[e], final_cnt[e])

            block.tensor(lambda e: run("pe", e))
            block.scalar(lambda e: run("act", e))
            block.vector(lambda e: run("dve", e))
            block.gpsimd(lambda e: run("pool", e))
            block.sync(lambda e: run("sp", e))


VEC_OFF = {}
_o = 0
for _n, _w in (("gn1", 32), ("gn2", 32), ("gq", 4), ("gk", 4), ("ggla", 4), ("psc", 16), ("sink", 16),
               ("cvec", 8), ("ctxb", 1), ("mres", 1), ("bmod", 192)):
    VEC_OFF[_n] = _o
    _o += _w
NVEC = _o

CF_TT, CF_SU, CF_SL, NCF = 0, 256, 384, 512
CB_RM, CB_MA, CB_AM, CB_BAND, CB_RC, CB_RS, NCB = 0, 128, 384, 896, 896 + 4096, 896 + 4096 + 1024, 896 + 4096 + 2048


def build_program():
    nc = bass.Bass("TRN2", target_bir_lowering=False)
    dt = lambda name, shape, kind="ExternalInput": nc.dram_tensor(name, list(shape), F32, kind=kind).ap()
    x_in = dt("x_in", [T, D])
    vecs_d = dt("vecs", [128, NVEC])
    cf_d = dt("cf", [128, NCF])
    cb_d = dt("cb", [128, NCB])
    w2_d = dt("w2aug", [64, L * 512])
    ctxk_d = dt("ctxk", [L, 256, 128])
    ctxv_d = dt("ctxv", [L, 256, 128])
    s0_d = dt("s0", [L, 2, 4, 64, 128])
    w_in = dt("w_in", [L, D, W_IN])
    w_pool_d = dt("w_pool", [L, 4, 128, 128])
    w_br = [dt("w_br_a", [L, 512, D]), dt("w_br_b", [L, 512, D]), dt("w_br_c", [L, 512, D])]
    w_out = dt("w_out", [L, D, D])
    w_mod = dt("w_mod", [L, D, 6 * D])
    w_fg = dt("w_ff_gate", [L, D, DFF])
    w_fu = dt("w_ff_up", [L, D, DFF])
    w_fd = dt("w_ff_down", [L, DFF, D])
    y_d = dt("y", [T, D], "ExternalOutput")
    nk_d = dt("nk", [L, T, 128], "ExternalOutput")
    nv_d = dt("nv", [L, T, 128], "ExternalOutput")
    nst_d = dt("nst", [L, 4, 2, 4, 64, 128], "ExternalOutput")

    off = [(nc.sbuf_base + 63) // 64 * 64]

    def alloc(name, shape, dtype, at=None):
        nbytes = int(np.prod(shape[1:])) * (4 if dtype == F32 else 2)
        nbytes = (nbytes + 31) // 32 * 32
        if at is None:
            o = off[0]
            off[0] += nbytes
        else:
            o = at
        return nc.alloc_sbuf_tensor_at(name, list(shape), dtype, offset=o)

    xT = alloc("xT", [128, 8, T], F32)
    hn = alloc("hn", [128, 8, T], BF16)
    ring = [alloc("ring%d" % i, [128, SLOTW], BF16) for i in range(NSLOT)]
    vecs = alloc("vecs_s", [128, NVEC], F32)
    cf = alloc("cf_s", [128, NCF], F32)
    cb = alloc("cb_s", [128, NCB], BF16)
    w2all = alloc("w2all", [64, 512], BF16)
    wpool = alloc("wpool", [128, 4, 128], BF16)
    ident = alloc("ident", [128, 128], F32)
    identb = alloc("identb", [128, 128], BF16)
    onesb = alloc("onesb", [128, 128], BF16)
    bones = alloc("bones", [128, 128], BF16)
    ones01 = alloc("ones01", [128, 2, 128], BF16)
    onef = alloc("onef", [1, 8], F32)
    aug2 = alloc("aug2", [64, T], BF16)
    scT = alloc("scT", [128, 8], BF16)
    modTs = [alloc("modT%d" % i, [128, 48], F32) for i in range(2)]
    modrow = alloc("modrow", [1, 512], F32)
    A12s = [alloc("A12_%d" % i, [128, 16], F32) for i in range(2)]
    esink = alloc("esink", [128, 16], F32)
    Gt = alloc("Gt", [128, 4, 16], F32)
    Sst = alloc("Sst", [128, 4, 128], F32)
    tmpf = [alloc("tmpf%d" % i, [128, 512], F32) for i in range(NTF)]
    tmpb = [alloc("tmpb%d" % i, [128, 512], BF16) for i in range(NTB)]
    a0 = off[0]
    QA = alloc("QA", [128, 4, T], BF16)
    KA = alloc("KA", [128, 2, T], BF16)
    KC = alloc("KC", [128, 2, 256], BF16)
    VP = alloc("VP", [128, 8, 2, 128], BF16)
    VPC = alloc("VPC", [128, 2, 2, 128], BF16)
    assert off[0] - a0 <= 18432
    off[0] = a0 + 18432
    c2 = off[0]
    kn = alloc("kn", [128, T], F32)
    oa = alloc("oa", [128, 4, T], BF16)
    c4 = off[0]
    QG = alloc("QG", [128, 4, T], BF16)
    KG = alloc("KG", [128, 4, T], BF16)
    a_end_ht = off[0]
    c6 = off[0]
    kbtok = alloc("kbtok", [128, 8, 256], BF16)
    c7 = off[0]
    Vg = alloc("Vg", [128, 8, 512], BF16)
    c8 = off[0]
    Kp = alloc("Kp", [128, 8, 4, 2, 64], BF16)
    c9 = off[0]
    ob = alloc("ob", [128, 4, T], BF16)
    S_all = alloc("S_all", [128, 16, 4, 128], BF16, at=a0)
    oc = alloc("oc", [128, 4, T], BF16, at=a0)
    uctok = alloc("uctok", [128, 8, 512], BF16, at=c8)
    stSs = [alloc("stSa", [128, 2, 4, 128], F32, at=c2), alloc("stSb", [128, 2, 4, 128], F32, at=c6)]
    stg = [alloc("stg0", [128, 1024], F32, at=c9), alloc("stg1", [128, 1024], F32, at=c7),
           alloc("stg2", [128, 1024], F32, at=c8), alloc("stg3", [128, 1024], F32, at=c9 + 4096)]
    cst = alloc("cst", [128, 2, 2, 128], F32, at=c6)
    mixed = alloc("mixed", [128, 8, T], BF16, at=c4)
    hT = alloc("hT", [128, NFT, T], BF16, at=a0)
    assert a_end_ht - a0 >= NFT * T * 2, (a_end_ht - a0)
    total_sbuf = off[0]
    ps = [nc.alloc_psum_tensor("ps%d" % i, [128, 512], F32) for i in range(8)]

    P = Prog(nc)
    P.alias("r0a", "QA", "S_all", "oc", "hT")
    P.alias("r0b", "KA", "S_all", "hT")
    P.alias("r0c", "KC", "S_all", "hT")
    P.alias("r0d", "VP", "S_all", "hT")
    P.alias("r0e", "VPC", "S_all", "hT")
    P.alias("r2", "kn", "stSa", "hT")
    P.alias("r3", "oa", "hT")
    P.alias("r4", "QG", "mixed", "hT")
    P.alias("r5", "KG", "mixed", "hT")
    P.alias("r6", "kbtok", "cst", "stSb")
    P.alias("r7", "Vg", "stg1")
    P.alias("r8", "Kp", "uctok", "stg2")
    P.alias("r9", "ob", "stg0", "stg3")

    bank_rr = [0]

    def nb():
        b = bank_rr[0]
        bank_rr[0] = (b + 1) % 8
        return b

    tf_rr = [0]
    tb_rr = [0]

    def ntf():
        i = tf_rr[0]
        tf_rr[0] = (i + 1) % NTF
        return i

    def ntb():
        i = tb_rr[0]
        tb_rr[0] = (i + 1) % NTB
        return i

    def mm(out, lhsT, rhs, start, stop, reads, writes):
        P.add("pe", lambda e: e.matmul(out, lhsT=lhsT, rhs=rhs, start=start, stop=stop), reads=reads, writes=writes)

    def tr(out, in_, reads, writes):
        P.add("pe", lambda e: e.transpose(out=out, in_=in_, identity=ident[:]), reads=list(reads) + ["ident"], writes=writes)

    def act(out, in_, func, reads, writes, bias=None, scale=None):
        kw = {}
        if bias is not None:
            kw["bias"] = bias
        if scale is not None:
            kw["scale"] = scale
        P.add("act", lambda e: e.activation(out=out, in_=in_, func=func, **kw), reads=reads, writes=writes)

    def tt(out, in0, in1, op, reads, writes, eng="dve"):
        P.add(eng, lambda e: e.tensor_tensor(out=out, in0=in0, in1=in1, op=op), reads=reads, writes=writes)

    def ts(out, in0, s1, s2, op0, op1, reads, writes):
        if s2 is None:
            P.add("dve", lambda e: e.tensor_scalar(out=out, in0=in0, scalar1=s1, scalar2=None, op0=op0), reads=reads, writes=writes)
        else:
            P.add("dve", lambda e: e.tensor_scalar(out=out, in0=in0, scalar1=s1, scalar2=s2, op0=op0, op1=op1), reads=reads, writes=writes)

    def stt(out, in0, scalar, in1, op0, op1, reads, writes):
        P.add("dve", lambda e: e.scalar_tensor_tensor(out=out, in0=in0, scalar=scalar, in1=in1, op0=op0, op1=op1), reads=reads, writes=writes)

    def cpy(out, in_, reads, writes, eng="dve"):
        if eng == "act":
            act(out, in_, AF.Copy, reads, writes)
        else:
            P.add("dve", lambda e: e.tensor_copy(out=out, in_=in_), reads=reads, writes=writes)

    def rcp(out, in_, reads, writes):
        P.add("dve", lambda e: e.reciprocal(out=out, in_=in_), reads=reads, writes=writes)

    def mset(ap, val, writes, eng="dve"):
        P.add(eng, lambda e: e.memset(ap, val), writes=writes)

    def dma(eng, out, in_, reads, writes, slot):
        P.add(eng, lambda e: e.dma_start(out=out, in_=in_), reads=reads, writes=writes, dma_slot=slot)

    ring_rr = [0]

    def wload(parts):
        s_ = ring_rr[0]
        ring_rr[0] = (s_ + 1) % NSLOT
        MAXP = 11
        assert len(parts) <= MAXP
        keys = [("w", s_, pi) for pi in range(MAXP)]
        for pi, (dfn, src) in enumerate(parts):
            wr = [("w", s_, pi)]
            if pi == 0:
                wr += [("w", s_, q) for q in range(len(parts), MAXP)]
            dma("pool", dfn(ring[s_]), src, [], wr, ("w", s_, pi))
        return ring[s_], keys

    def V(name, l=None, width=1, idx=0):
        o = VEC_OFF[name] + idx
        return vecs[:, o:o + width]

    dma("sp", vecs[:], vecs_d, [], ["vecs"], "vecs")
    dma("sp", cf[:], cf_d, [], ["cf"], "cf")
    P.add("pool", lambda e: e.memset(ident[:], 1.0), writes=["ident"])
    P.add("pool", lambda e: e.affine_select(out=ident[:], in_=ident[:], pattern=[[-1, 128]], compare_op=ALU.is_equal,
                                            fill=0.0, base=0, channel_multiplier=1), reads=["ident"], writes=["ident"])
    cpy(identb[:], ident[:], ["ident"], ["identb"])
    mset(onesb[:], 1.0, ["onesb"])
    mset(bones[:], 0.0, ["bones"])
    mset(bones[0:64, 0:64], 1.0, ["bones"])
    mset(bones[64:128, 64:128], 1.0, ["bones"])
    mset(ones01[:], 0.0, ["ones01"])
    mset(ones01[:, 0, 0:64], 1.0, ["ones01"])
    mset(ones01[:, 1, 64:128], 1.0, ["ones01"])
    mset(onef[:], 1.0, ["onef"])
    mset(aug2[:], 1.0, ["aug2"])
    act(scT[:], V("cvec", width=8), AF.Silu, ["vecs"], ["scT"])
    act(esink[:], V("sink", width=16), AF.Exp, ["vecs"], ["esink"])

    for blk in range(0 if DBG.get('skipx') else NB):
        st = stg[blk % 4]
        dma("sp", st[:], x_in[blk * 128:(blk + 1) * 128, :], [], [("stg%d" % (blk % 4),)], ("stg%d" % (blk % 4),))
        for k4 in range(2):
            b = nb()
            for kk in range(4):
                k = k4 * 4 + kk
                tr(ps[b][:, kk * 128:(kk + 1) * 128], st[:, k * 128:(k + 1) * 128], [("stg%d" % (blk % 4),)], [("ps", b)])
            cpy(xT[:, k4 * 4:(k4 + 1) * 4, blk * 128:(blk + 1) * 128], ps[b][:].rearrange("p (a t) -> p a t", a=4),
                [("ps", b)], [("xT", k4 * 4 + kk, blk // 4) for kk in range(4)], eng=("act" if k4 == 0 else "dve"))

    THS = [slice(0, 512), slice(512, 1024)]

    def norm_mod(acol, shcol, ak_, mk_, filler=None):
        tfrs = []
        for th in range(2):
            b = nb()
            for k in range(8):
                tb = ntb()
                if k % 4 != 3:
                    tt(tmpb[tb][:], xT[:, k, THS[th]], xT[:, k, THS[th]], ALU.mult, [("xT", k, th)], [("tmpb", tb)])
                else:
                    act(tmpb[tb][:], xT[:, k, THS[th]], AF.Square, [("xT", k, th)], [("tmpb", tb)])
                mm(ps[b][:], onesb[:], tmpb[tb][:], k == 0, k == 7, [("tmpb", tb), "onesb"], [("ps", b)])
            tfr = ntf()
            act(tmpf[tfr][:], ps[b][:], AF.Ln, [("ps", b)], [("tmpf", tfr)], bias=EPS, scale=1.0 / D)
            act(tmpf[tfr][:], tmpf[tfr][:], AF.Exp, [("tmpf", tfr)], [("tmpf", tfr)], scale=-0.5)
            tfrs.append(tfr)
        if filler is not None:
            filler()
        for th in range(2):
            tfr = tfrs[th]
            for k in range(8):
                tf2 = ntf()
                while tf2 in tfrs:
                    tf2 = ntf()
                stt(tmpf[tf2][:], xT[:, k, THS[th]], acol(k), tmpf[tfr][:], ALU.mult, ALU.mult,
                    [("xT", k, th), ("tmpf", tfr), ak_, mk_], [("tmpf", tf2)])
                act(hn[:, k, THS[th]], tmpf[tf2][:], AF.Identity, [("tmpf", tf2), mk_], [("hn", k, th)], bias=shcol(k), scale=1.0)

    def proj_fm(lhs_fn, wkeys, cb_fn, kt=8, rhs_src=None, rhs_keys=None):
        for th in range(2):
            b = nb()
            for k in range(kt):
                if rhs_src is None:
                    r, rk = hn[:, k, THS[th]], [("hn", k, th)]
                else:
                    r, rk = rhs_src(k, th), rhs_keys(k, th)
                mm(ps_out(b, lhs_fn(k)), lhs_fn(k), r, k == 0, k == kt - 1, list(wkeys) + rk, [("ps", b)])
            cb_fn(b, th)

    def ps_out(b, lhs):
        m = lhs.shape[-1]
        return ps[b][0:m, :]

    def proj_tm(rhs_fn, n, wkeys, cb_fn):
        for blk in range(NB):
            b = nb()
            for k in range(8):
                mm(ps[b][:, 0:n], hn[:, k, blk * 128:(blk + 1) * 128], rhs_fn(k), k == 0, k == 7,
                   list(wkeys) + [("hn", k, blk // 4)], [("ps", b)])
            cb_fn(b, blk)

    def wsrc(w, l, c0, n):
        return w[l, :, c0:c0 + n].rearrange("(k p) c -> p k c", p=128)

    def v3(slot, kt, c):
        return slot[:, 0:kt * c].rearrange("p (k c) -> p k c", c=c)

    def mod_panels(l, n0, n1):
        modT = modTs[l % 2]
        mk = ("modT", l % 2)
        for n in range(n0, n1):
            slot, wk = wload([(lambda s: v3(s, 8, 512), wsrc(w_mod, l, n * 512, 512))])
            pw = v3(slot, 8, 512)
            b = nb()
            for k in range(8):
                mm(ps[b][0:1, :], scT[:, k:k + 1], pw[:, k, :], k == 0, k == 7, wk + ["scT"], [("ps", b)])
            cpy(modrow[:], ps[b][0:1, :], [("ps", b)], ["modrow"], eng="act")
            b2 = nb()
            for j in range(4):
                mm(ps[b2][:, j:j + 1], modrow[0:1, j * 128:(j + 1) * 128], onef[0:1, 0:1], True, True, ["modrow", "onef"], [("ps", b2)])
            tt(modT[:, n * 4:(n + 1) * 4], ps[b2][:, 0:4], V("bmod", width=4, idx=l * 48 + n * 4), ALU.add,
               [("ps", b2), "vecs"], [mk])

    def mod_finish(l, part=3):
        modT = modTs[l % 2]
        A12 = A12s[l % 2]
        mk = ("modT", l % 2)
        ak = ("A12", l % 2)
        if part & 1:
            ts(A12[:, 0:8], modT[:, 8:16], 1.0, None, ALU.add, None, [mk], [(ak, 0)])
            tt(A12[:, 0:8], A12[:, 0:8], V("gn1", width=8, idx=l * 8), ALU.mult, [(ak, 0), "vecs"], [(ak, 0)])
        if part & 2:
            ts(A12[:, 8:16], modT[:, 32:40], 1.0, None, ALU.add, None, [mk], [(ak, 1)])
            tt(A12[:, 8:16], A12[:, 8:16], V("gn2", width=8, idx=l * 8), ALU.mult, [(ak, 1), "vecs"], [(ak, 1)])

    def layer(l):
        dma("pool", w2all[:], w2_d[:, l * 512:(l + 1) * 512], [], ["w2all"], "w2all")
        dma("pool", wpool[:], w_pool_d[l].rearrange("g i o -> i g o"), [], ["wpool"], "wpool")
        modT = modTs[l % 2]
        A12 = A12s[l % 2]
        mk_ = ("modT", l % 2)
        ak_ = ("A12", l % 2)
        if DBG.get('stop') == 'mod':
            return
        norm_mod(lambda k: A12[:, k:k + 1], lambda k: modT[:, k:k + 1], (ak_, 0), mk_)

        if DBG.get('stop') == 'norm1':
            return
        cm = DBG.get('ctxmask', 15)
        if cm & 1:
            dma("sp", cst[:, 0, :, :], ctxk_d[l].rearrange("(b p) c -> p b c", p=128), [], [("cst", 0)], ("cst", 0))
            dma("sp", cst[:, 1, :, :], ctxv_d[l].rearrange("(b p) c -> p b c", p=128), [], [("cst", 1)], ("cst", 1))
        if cm & 2:
            mset(VP[:, :, 0, 64:128], 0.0, [("VP", "z0")])
            mset(VP[:, :, 1, 0:64], 0.0, [("VP", "z1")])
            mset(VPC[:, :, 0, 64:128], 0.0, [("VPC", "z0")])
            mset(VPC[:, :, 1, 0:64], 0.0, [("VPC", "z1")])
            mset(KA[64:128, 0, :], 0.0, [("KA", "z0")])
            mset(KA[0:64, 1, :], 0.0, [("KA", "z1")])
            mset(KC[64:128, 0, :], 0.0, [("KC", "z0")])
            mset(KC[0:64, 1, :], 0.0, [("KC", "z1")])
        if cm & 4:
            b = nb()
            for cbk in range(2):
                tr(ps[b][:, cbk * 128:(cbk + 1) * 128], cst[:, 0, cbk, :], [("cst", 0)], [("ps", b)])
            cpy(KC[0:64, 0, :], ps[b][0:64, 0:256], [("ps", b)], [("KC", "v0")], eng="act")
            cpy(KC[64:128, 1, :], ps[b][64:128, 0:256], [("ps", b)], [("KC", "v1")], eng="act")
        if cm & 8:
            cpy(VPC[:, :, 0, 0:64], cst[:, 1, :, 0:64], [("cst", 1)], [("VPC", "v0")])
            cpy(VPC[:, :, 1, 64:128], cst[:, 1, :, 64:128], [("cst", 1)], [("VPC", "v1")])
        if DBG.get('stop') == 'ctx':
            return
        slot, wk0 = wload([
            (lambda s, g=g, j=j: s[:, 0:8 * 512].rearrange("p (k j g d) -> p k j g d", k=8, j=4, g=2)[:, :, j, g, :],
             w_in[l, :, g * 256 + j * 64:g * 256 + (j + 1) * 64].rearrange("(k p) d -> p k d", p=128)) for g in range(2) for j in range(4)])
        pw = v3(slot, 8, 512)
        slot, wk = wload([(lambda s: v3(s, 8, 512)[:, :, 0:256], wsrc(w_in, l, 512, 256)),
                          (lambda s: v3(s, 8, 512)[:, :, 256:512], wsrc(w_in, l, 1024, 256))])
        pw1 = v3(slot, 8, 512)
        qk_items = []
        for j in range(4):
            for th in range(2):
                qk_items.append((lambda k, j=j: pw[:, k, j * 128:(j + 1) * 128], wk0, th, QA[:, j, THS[th]], [("QA", j, th)], V("gq", idx=l), False))
        for th in range(2):
            qk_items.append((lambda k: pw1[:, k, 0:128], wk, th, None, [("KA", th, 0), ("KA", th, 1)], V("gk", idx=l), True))
        qst = {}

        def qkA(i):
            lhs_fn, wkeys, th, dst, dkey, gcol, keep = qk_items[i]
            b_ = nb()
            for k in range(8):
                mm(ps[b_][:], lhs_fn(k), hn[:, k, THS[th]], k == 0, k == 7, list(wkeys) + [("hn", k, th)], [("ps", b_)])
            tb = i % 2
            act(tmpb[tb][:], ps[b_][:], AF.Square, [("ps", b_)], [("tmpb", tb)])
            qst[i] = b_

        def qkB(i):
            lhs_fn, wkeys, th, dst, dkey, gcol, keep = qk_items[i]
            b_ = qst[i]
            tb = i % 2
            base = 4 * (i % 2)
            b2 = nb()
            mm(ps[b2][:], bones[:], tmpb[tb][:], True, True, [("tmpb", tb), "bones"], [("ps", b2)])
            t1 = base
            act(tmpf[t1][:], ps[b2][:], AF.Ln, [("ps", b2)], [("tmpf", t1)], bias=EPS, scale=1.0 / 64)
            act(tmpf[t1][:], tmpf[t1][:], AF.Exp, [("tmpf", t1)], [("tmpf", t1)], scale=-0.5)
            t2 = base + 1
            tb2 = 2 + i % 2
            if keep:
                dstn = kn[:, THS[th]]
                kkey = [("kn", th)]
                stt(dstn, ps[b_][:], gcol, tmpf[t1][:], ALU.mult, ALU.mult, [("ps", b_), ("tmpf", t1), "vecs"], kkey)
                cpy(tmpb[tb2][:], dstn, kkey, [("tmpb", tb2)], eng="act")
            else:
                stt(tmpb[tb2][:], ps[b_][:], gcol, tmpf[t1][:], ALU.mult, ALU.mult, [("ps", b_), ("tmpf", t1), "vecs"], [("tmpb", tb2)])

        def qkC(i):
            lhs_fn, wkeys, th, dst, dkey, gcol, keep = qk_items[i]
            base = 4 * (i % 2)
            tb2 = 2 + i % 2
            dstn = kn[:, THS[th]] if keep else tmpb[tb2][:]
            kkey = [("kn", th)] if keep else [("tmpb", tb2)]
            b3 = nb()
            mm(ps[b3][:], cb[:, CB_RM:CB_RM + 128], tmpb[tb2][:], True, True, [("tmpb", tb2), "cb"], [("ps", b3)])
            t3, t4 = base + 2, base + 3
            tt(tmpf[t3][:], ps[b3][:], cb[:, CB_RS + th * 512:CB_RS + (th + 1) * 512], ALU.mult, [("ps", b3), "cb"], [("tmpf", t3)])
            tt(tmpf[t4][:], dstn, cb[:, CB_RC + th * 512:CB_RC + (th + 1) * 512], ALU.mult, kkey + ["cb"], [("tmpf", t4)])
            if dst is None:
                tt(KA[0:64, 0, THS[th]], tmpf[t4][0:64, :], tmpf[t3][0:64, :], ALU.add, [("tmpf", t3), ("tmpf", t4)], [dkey[0]])
                tt(KA[64:128, 1, THS[th]], tmpf[t4][64:128, :], tmpf[t3][64:128, :], ALU.add, [("tmpf", t3), ("tmpf", t4)], [dkey[1]])
            else:
                tt(dst, tmpf[t4][:], tmpf[t3][:], ALU.add, [("tmpf", t3), ("tmpf", t4)], dkey)

        def tm_block(blk, c0, n, cb_fn):
            b_ = nb()
            for k in range(8):
                mm(ps[b_][:, 0:n], hn[:, k, blk * 128:(blk + 1) * 128], pw1[:, k, c0:c0 + n], k == 0, k == 7,
                   list(wk) + [("hn", k, blk // 4)], [("ps", b_)])
            cb_fn(b_, blk)

        def va_cb(b, blk):
            cpy(stg[0][:, blk * 128:(blk + 1) * 128], ps[b][:, 0:128], [("ps", b)], [("stg0",)], eng="act")
            cpy(VP[:, blk, 0, 0:64], ps[b][:, 0:64], [("ps", b)], [("VP", blk, 0)])
            cpy(VP[:, blk, 1, 64:128], ps[b][:, 64:128], [("ps", b)], [("VP", blk, 1)])

        def kbt_cb(b, blk):
            cpy(kbtok[:, blk, :], ps[b][:, 0:256], [("ps", b)], [("kbtok", blk)], eng="act")

        NQ = len(qk_items)
        for st_ in range(NQ + 2):
            if st_ < NQ:
                qkA(st_)
            if 0 <= st_ - 1 < NQ:
                qkB(st_ - 1)
            if 0 <= st_ - 2 < NQ:
                qkC(st_ - 2)
            if st_ < NB:
                tm_block(st_, 128, 128, va_cb)
                tm_block(st_, 256, 256, kbt_cb)
        tf_rr[0] = 0
        tb_rr[0] = 0

        if DBG.get('stop') == 'ap2':
            return
        if not DBG.get("nonv"):
            dma("sp", nv_d[l].rearrange("(b p) c -> p b c", p=128), stg[0][:].rearrange("p (b c) -> p b c", c=128),
                [("stg0",)], [], ("ystg0",))

        if DBG.get('stop') == 'ap3':
            return
        if DBG.get('stop') == 'ap4':
            return
        for k4 in range(2):
            b = nb()
            for kk in range(4):
                blk = k4 * 4 + kk
                tr(ps[b][:, kk * 128:(kk + 1) * 128], kn[:, blk * 128:(blk + 1) * 128], [("kn", k4)], [("ps", b)])
            cpy(stg[1][:, k4 * 512:(k4 + 1) * 512], ps[b][:], [("ps", b)], [("stg1",)], eng="act")
        dma("sp", nk_d[l].rearrange("(b p) c -> p b c", p=128), stg[1][:].rearrange("p (b c) -> p b c", c=128),
            [("stg1",)], [], ("ystg1",))

        if DBG.get('stop') == 'attnproj':
            return
        for i in range(NB):
            ob_, db_ = (4, 5) if i % 2 == 0 else (6, 7)
            kbs = []
            if i > 0:
                kbs.append(("loc", i - 1, 0 if i % 2 == 0 else 1))
            kbs.append(("loc", i, None))
            if i < NB - 1:
                kbs.append(("loc", i + 1, 2 if i % 2 == 0 else 3))
            kbs.append(("ctx", 0, None))
            kbs.append(("ctx", 1, None))
            steps = [(kind, kb, mk, g) for (kind, kb, mk) in kbs for g in range(2)]
            nmm = len(steps)

            def srcs(n):
                kind, kb, mk, g = steps[n]
                gs = slice(g * 64, (g + 1) * 64)
                if kind == "loc":
                    return (KA[:, g, kb * 128:(kb + 1) * 128], [("KA", kb // 4, g), ("KA", "z%d" % g)], VP[:, kb, g, :],
                            [("VP", kb, g), ("VP", "z%d" % g)], None, gs, mk, g)
                return (KC[:, g, kb * 128:(kb + 1) * 128], [("KC", "v%d" % g), ("KC", "z%d" % g)], VPC[:, kb, g, :],
                        [("VPC", "v%d" % g), ("VPC", "z%d" % g)], V("ctxb"), gs, mk, g)

            def emit_st(n):
                lk, lkey, vsrc, vkey, bias, gs, mk, g = srcs(n)
                sb = n % 4
                mm(ps[sb][:], lk, QA[:, :, i * 128:(i + 1) * 128], True, mk is None,
                   lkey + [("QA", j, i // 4) for j in range(4)], [("ps", sb)])
                if mk is not None:
                    mm(ps[sb][:], identb[:], cb[:, CB_AM + mk * 128:CB_AM + (mk + 1) * 128].unsqueeze(1).to_broadcast([128, 4, 128]),
                       False, True, ["identb", "cb"], [("ps", sb)])

            def emit_rest(n):
                lk, lkey, vsrc, vkey, bias, gs, mk, g = srcs(n)
                sb = n % 4
                tb = ntb()
                if bias is None:
                    act(tmpb[tb][:], ps[sb][:], AF.Exp, [("ps", sb)], [("tmpb", tb)], scale=0.125)
                else:
                    act(tmpb[tb][:], ps[sb][:], AF.Exp, [("ps", sb), "vecs"], [("tmpb", tb)], scale=0.125, bias=bias)
                mm(ps[ob_][:], vsrc, tmpb[tb][:], n == 0, n == nmm - 1, vkey + [("tmpb", tb)], [("ps", ob_)])
                mm(ps[db_][:], ones01[:, g, :], tmpb[tb][:], n == 0, n == nmm - 1, ["ones01", ("tmpb", tb)], [("ps", db_)])

            emit_st(0)
            emit_st(1)
            emit_st(2)
            for n in range(nmm):
                if n + 3 < nmm:
                    emit_st(n + 3)
                emit_rest(n)
            t1 = ntf()
            tt(tmpf[t1][:].rearrange("p (j q) -> p j q", j=4), ps[db_][:].rearrange("p (j q) -> p j q", j=4),
               esink[:, l * 4:(l + 1) * 4].unsqueeze(2).to_broadcast([128, 4, 128]), ALU.add, [("ps", db_), "esink"], [("tmpf", t1)])
            act(tmpf[t1][:], tmpf[t1][:], AF.Ln, [("tmpf", t1)], [("tmpf", t1)])
            act(tmpf[t1][:], tmpf[t1][:], AF.Exp, [("tmpf", t1)], [("tmpf", t1)], scale=-1.0)
            tt(oa[:, :, i * 128:(i + 1) * 128], ps[ob_][:].rearrange("p (j q) -> p j q", j=4),
               tmpf[t1][:].rearrange("p (j q) -> p j q", j=4), ALU.mult, [("ps", ob_), ("tmpf", t1)], [("oa", i)])
            mod_panels(l, 4 + i, 5 + i)
        mod_finish(l, part=2)
        bank_rr[0] = 0

        if DBG.get('stop') == 'attn':
            return
        slot, wk = wload([
            (lambda s, r=r, h=h: s[:, 0:8 * 512].rearrange("p (k h r d) -> p k h r d", k=8, h=4, r=2)[:, :, h, r, :],
             w_in[l, :, 768 + h * 64:768 + (h + 1) * 64].rearrange("(k p) d -> p k d", p=128)) for r in range(2) for h in range(4)])
        pwq = v3(slot, 8, 512)
        for h in range(4):
            proj_fm(lambda k, h=h: pwq[:, k, h * 128:(h + 1) * 128], wk,
                    lambda b, th, h=h: act(QG[:, h, THS[th]], ps[b][:], AF.Identity, [("ps", b)], [("QG", h, th)], scale=0.125))
        slot, wk = wload([
            (lambda s, r=r, h=h: s[:, 0:8 * 560].rearrange("p (k c) -> p k c", c=560)[:, :, h * 128 + r * 64:h * 128 + (r + 1) * 64],
             w_in[l, :, 1024 + h * 64:1024 + (h + 1) * 64].rearrange("(k p) d -> p k d", p=128)) for r in range(2) for h in range(4)] + [
            (lambda s: s[:, 0:8 * 560].rearrange("p (k c) -> p k c", c=560)[:, :, 512:528], wsrc(w_in, l, 2304, 16)),
            (lambda s: s[:, 0:8 * 560].rearrange("p (k c) -> p k c", c=560)[:, :, 528:544], wsrc(w_in, l, 2304, 16)),
            (lambda s: s[:, 0:8 * 560].rearrange("p (k c) -> p k c", c=560)[:, :, 544:560], wsrc(w_in, l, 2320, 16))])
        pwk = slot[:, 0:8 * 560].rearrange("p (k c) -> p k c", c=560)
        for h in range(4):
            proj_fm(lambda k, h=h: pwk[:, k, h * 128:(h + 1) * 128], wk,
                    lambda b, th, h=h: cpy(KG[:, h, THS[th]], ps[b][:], [("ps", b)], [("KG", h, th)]))

        def gl_cb(b, th):
            cpy(aug2[0:16, THS[th]], ps[b][0:16, :], [("ps", b)], [("aug2", th, 0)], eng="act")
            cpy(aug2[32:48, THS[th]], ps[b][32:48, :], [("ps", b)], [("aug2", th, 1)])
        proj_fm(lambda k: pwk[:, k, 512:560], wk, gl_cb)
        slot, wk_vb = wload([(lambda s: v3(s, 8, 512), wsrc(w_in, l, 1280, 512))])
        pwv = v3(slot, 8, 512)

        def vb_block(blk):
            b_ = nb()
            for k in range(8):
                mm(ps[b_][:, 0:512], hn[:, k, blk * 128:(blk + 1) * 128], pwv[:, k, :], k == 0, k == 7,
                   list(wk_vb) + [("hn", k, blk // 4)], [("ps", b_)])
            cpy(Vg[:, blk, :], ps[b_][:], [("ps", b_)], [("Vg", blk)])

        if DBG.get('stop') == 'glaproj':
            return
        TTm = cf[:, CF_TT:CF_TT + 256]

        def preA(blk):
            th = blk // 4
            bs = slice(blk * 128, (blk + 1) * 128)
            base = 4 * (blk % 2)
            b_ = nb()
            mm(ps[b_][:], aug2[:, bs], w2all[:, :], True, True,
               [("aug2", th, 0), ("aug2", th, 1), "aug2", "w2all"], [("ps", b_)])
            te, tl = base, base + 1
            act(tmpf[te][:], ps[b_][:], AF.Exp, [("ps", b_)], [("tmpf", te)], scale=-1.0)
            act(tmpf[tl][:], tmpf[te][:], AF.Ln, [("tmpf", te)], [("tmpf", tl)], bias=1.0, scale=1.0)

        def preB(blk):
            th = blk // 4
            bs = slice(blk * 128, (blk + 1) * 128)
            base = 4 * (blk % 2)
            tl, td, tq, tk = base + 1, base + 2, base + 3, base
            lt4 = tmpf[tl][:].rearrange("p (h r d) -> p h r d", h=4, r=2)
            b_ = nb()
            mm(ps[b_][:, 0:256], cf[:, CF_SU:CF_SU + 128], lt4[:, :, 0, :], True, True, [("tmpf", tl), "cf"], [("ps", b_)])
            mm(ps[b_][:, 256:512], cf[:, CF_SL:CF_SL + 128], lt4[:, :, 1, :], True, True, [("tmpf", tl), "cf"], [("ps", b_)])
            bb = [nb(), nb()]
            for h in range(4):
                mm(ps[bb[h // 2]][:, (h % 2) * 256:(h % 2 + 1) * 256], tmpf[tl][:, h * 128:(h + 1) * 128], TTm, True, True,
                   [("tmpf", tl), "cf"], [("ps", bb[h // 2])])
            act(tmpf[td][:], ps[b_][:], AF.Exp, [("ps", b_)], [("tmpf", td)], scale=-1.0 / 16)
            for r in range(2):
                tt(Kp[:, blk, :, r, :], kbtok[:, blk, :].rearrange("p (h d) -> p h d", h=4),
                   tmpf[td][:, r * 256:(r + 1) * 256].rearrange("p (h d) -> p h d", h=4), ALU.mult,
                   [("kbtok", blk), ("tmpf", td)], [("Kp", blk, r)])
            Eq = tmpf[tq][:].rearrange("p (h t) -> p h t", h=4)
            Ek = tmpf[tk][:].rearrange("p (h t) -> p h t", h=4)
            for hb in range(2):
                pin = ps[bb[hb]][:].rearrange("p (h c) -> p h c", h=2)
                hs = slice(hb * 2, hb * 2 + 2)
                act(Eq[0:64, hs, :], pin[0:64, :, 0:128], AF.Exp, [("ps", bb[hb])], [("tmpf", tq)], scale=-1.0 / 16)
                act(Eq[64:128, hs, :], pin[64:128, :, 128:256], AF.Exp, [("ps", bb[hb])], [("tmpf", tq)], scale=-1.0 / 16)
                act(Ek[0:64, hs, :], pin[0:64, :, 0:128], AF.Exp, [("ps", bb[hb])], [("tmpf", tk)], scale=1.0 / 16)
                act(Ek[64:128, hs, :], pin[64:128, :, 128:256], AF.Exp, [("ps", bb[hb])], [("tmpf", tk)], scale=1.0 / 16)
            tt(QG[:, :, bs], QG[:, :, bs], Eq, ALU.mult, [("QG", h, th) for h in range(4)] + [("tmpf", tq)], [("QG", h, th) for h in range(4)])
            tt(KG[:, :, bs], KG[:, :, bs], Ek, ALU.mult, [("KG", h, th) for h in range(4)] + [("tmpf", tk)], [("KG", h, th) for h in range(4)])
            for cc in range(2):
                c = 2 * blk + cc
                cpy(Gt[0:64, :, c:c + 1], Eq[0:64, :, cc * 64 + 63:cc * 64 + 64], [("tmpf", tq)], ["Gt"])
                cpy(Gt[64:128, :, 15 - c:16 - c], Eq[64:128, :, cc * 64:cc * 64 + 1], [("tmpf", tq)], ["Gt"])

        for st_ in range(NB + 1):
            if st_ < NB:
                preA(st_)
            if st_ >= 1:
                preB(st_ - 1)
            if st_ < NB:
                vb_block(st_)
        tf_rr[0] = 0

        if DBG.get('stop') == 'glapre':
            return
        dma("sp", Sst[0:64, :, :], s0_d[l, 0].rearrange("h d e -> d h e"), [], [("Sst", 0)], ("Sst", 0))
        dma("sp", Sst[64:128, :, :], s0_d[l, 1].rearrange("h d e -> d h e"), [], [("Sst", 1)], ("Sst", 1))
        slot, wk_rb = wload([(lambda s: v3(s, 8, 512), wsrc(w_in, l, 1792, 512))])
        pwr = v3(slot, 8, 512)

        def rb_group(gi):
            h, th = gi // 2, gi % 2
            b_ = nb()
            for k in range(8):
                mm(ps[b_][:], pwr[:, k, h * 128:(h + 1) * 128], hn[:, k, THS[th]], k == 0, k == 7, wk_rb + [("hn", k, th)], [("ps", b_)])
            act(ob[:, h, THS[th]], ps[b_][:], AF.Silu, [("ps", b_)], [("ob", "rb", h, th)])

        for n in range(16):
            cfw, cbw = n, 15 - n
            if n % 2 == 1:
                rb_group(n // 2)
            cpy(S_all[:, n, :, :], Sst[:, :, :], [("Sst", 0), ("Sst", 1)], [("S_all", n)], eng=("dve" if n % 2 == 0 else "act"))
            kvt = n % 4
            for r, c in ((0, cfw), (1, cbw)):
                b = nb()
                prt = slice((c % 2) * 64, (c % 2) * 64 + 64)
                for h in range(4):
                    mm(ps[b][:, h * 128:(h + 1) * 128], Kp[prt, c // 2, h, :, :].rearrange("p r d -> p (r d)"),
                       Vg[prt, c // 2, h * 128:(h + 1) * 128], True, True,
                       [("Kp", c // 2, 0), ("Kp", c // 2, 1), ("Vg", c // 2)], [("ps", b)])
                rs = slice(r * 64, (r + 1) * 64)
                act(tmpf[kvt][rs, :], ps[b][rs, :], AF.Copy, [("ps", b)], [("tmpf", kvt)])
            tt(Sst[:, :, :], Sst[:, :, :], Gt[:, :, n:n + 1].to_broadcast([128, 4, 128]), ALU.mult,
               [("Sst", 0), ("Sst", 1), "Gt"], [("Sst", 0), ("Sst", 1)])
            tt(Sst[:, :, :], Sst[:, :, :], tmpf[kvt][:].rearrange("p (h e) -> p h e", h=4), ALU.add,
               [("Sst", 0), ("Sst", 1), ("tmpf", kvt)], [("Sst", 0), ("Sst", 1)])
            if (n + 1) % 4 == 0:
                sqf, sqb = cfw // 4, cbw // 4
                cpy(stSs[sqf // 2][0:64, sqf % 2, :, :], Sst[0:64, :, :], [("Sst", 0)], [("stS" + "ab"[sqf // 2], 0, sqf % 2)], eng="act")
                cpy(stSs[sqb // 2][64:128, sqb % 2, :, :], Sst[64:128, :, :], [("Sst", 1)], [("stS" + "ab"[sqb // 2], 1, sqb % 2)], eng="act")
                if n < 15:
                    ts(Sst[:].rearrange("p h e -> p (h e)"), Sst[:].rearrange("p h e -> p (h e)"), V("mres"), None, ALU.mult, None,
                       [("Sst", 0), ("Sst", 1), "vecs"], [("Sst", 0), ("Sst", 1)])
        for r in range(2):
            for sq in range(4):
                dma("sp", nst_d[l, sq, r].rearrange("h d e -> d h e"), stSs[sq // 2][r * 64:(r + 1) * 64, sq % 2, :, :],
                    [("stS" + "ab"[sq // 2], r, sq % 2)], [], ("stS", r, sq))

        if DBG.get('stop') == 'scan':
            return
        if DBG.get('stop') == 'rb':
            return
        maskA = cb[:, CB_MA:CB_MA + 256].rearrange("p (r t) -> p r t", r=2)
        gst = {}

        def goA(blk):
            th = blk // 4
            bs = slice(blk * 128, (blk + 1) * 128)
            for r in range(2):
                rs = slice(r * 64, (r + 1) * 64)
                tat = 2 * (blk % 2) + r
                b_ = nb()
                pa = ps[b_][:].rearrange("p (h t) -> p h t", h=4)
                for h in range(4):
                    mm(pa[:, h, :], KG[rs, h, bs], QG[rs, h, bs], True, True, [("KG", h, th), ("QG", h, th)], [("ps", b_)])
                tt(tmpb[tat][:].rearrange("p (h t) -> p h t", h=4), pa, maskA[:, r, :].unsqueeze(1).to_broadcast([128, 4, 128]),
                   ALU.mult, [("ps", b_), "cb"], [("tmpb", tat)])

        def goB(blk):
            th = blk // 4
            base = 4 * (blk % 2)
            bo = [nb(), nb()]
            for r in range(2):
                rs = slice(r * 64, (r + 1) * 64)
                tat = 2 * (blk % 2) + r
                po = ps[bo[r]][:].rearrange("p (h t) -> p h t", h=4)
                at3 = tmpb[tat][:].rearrange("p (h t) -> p h t", h=4)
                for h in range(4):
                    mm(po[:, h, :], Vg[:, blk, h * 128:(h + 1) * 128], at3[:, h, :], True, False, [("Vg", blk), ("tmpb", tat)], [("ps", bo[r])])
                    for cc in range(2):
                        c = 2 * blk + cc
                        tsl = slice(blk * 128 + cc * 64, blk * 128 + cc * 64 + 64)
                        si = c if r == 0 else 15 - c
                        mm(po[:, h, cc * 64:(cc + 1) * 64], S_all[rs, si, h, :], QG[rs, h, tsl], False, cc == 1,
                           [("S_all", si), ("QG", h, th)], [("ps", bo[r])])
            tc_, to_ = base, base + 1
            act(tmpf[tc_][:], ps[bo[1]][:], AF.Copy, [("ps", bo[1])], [("tmpf", tc_)])
            tt(tmpf[to_][:], ps[bo[0]][:], tmpf[tc_][:], ALU.add, [("ps", bo[0]), ("tmpf", tc_)], [("tmpf", to_)])
            tb = 4 + blk % 2
            act(tmpb[tb][:], tmpf[to_][:], AF.Square, [("tmpf", to_)], [("tmpb", tb)])

        def goC(blk):
            th = blk // 4
            bs = slice(blk * 128, (blk + 1) * 128)
            base = 4 * (blk % 2)
            to_, t1, t2 = base + 1, base + 2, base + 3
            tb = 4 + blk % 2
            b2 = nb()
            mm(ps[b2][:], onesb[:], tmpb[tb][:], True, True, [("tmpb", tb), "onesb"], [("ps", b2)])
            act(tmpf[t1][:], ps[b2][:], AF.Ln, [("ps", b2)], [("tmpf", t1)], bias=EPS, scale=1.0 / 128)
            act(tmpf[t1][:], tmpf[t1][:], AF.Exp, [("tmpf", t1)], [("tmpf", t1)], scale=-0.5)
            stt(tmpf[t2][:], tmpf[to_][:], V("ggla", idx=l), tmpf[t1][:], ALU.mult, ALU.mult, [("tmpf", to_), ("tmpf", t1), "vecs"], [("tmpf", t2)])
            tt(ob[:, :, bs], tmpf[t2][:].rearrange("p (h t) -> p h t", h=4), ob[:, :, bs], ALU.mult,
               [("tmpf", t2)] + [("ob", "rb", h, th) for h in range(4)], [("ob", blk)])

        slot, wk_uc = wload([(lambda s: v3(s, 8, 512), wsrc(w_in, l, 2336, 512))])
        pwu = v3(slot, 8, 512)

        def uc_block(blk):
            b_ = nb()
            for k in range(8):
                mm(ps[b_][:, 0:512], hn[:, k, blk * 128:(blk + 1) * 128], pwu[:, k, :], k == 0, k == 7,
                   list(wk_uc) + [("hn", k, blk // 4)], [("ps", b_)])
            cpy(uctok[:, blk, :], ps[b_][:], [("ps", b_)], [("uctok", blk)], eng=("act" if blk % 2 == 0 else "dve"))

        for st_ in range(NB + 2):
            if st_ < NB:
                goA(st_)
            if 0 <= st_ - 1 < NB:
                goB(st_ - 1)
            if 0 <= st_ - 2 < NB:
                goC(st_ - 2)
            if st_ < NB:
                uc_block(st_)
        tf_rr[0] = 0
        tb_rr[0] = 0

        if DBG.get('stop') == 'glaout':
            return
        def band(g, kind):
            o = CB_BAND + (g * 8 + kind) * 128
            return cb[:, o:o + 128]
        for g in range(4):
            for th in range(2):
                b = nb()
                for ii in range(4):
                    i = th * 4 + ii
                    terms = []
                    if i > 0:
                        terms.append((i - 1, 4 if i % 2 == 1 else 5))
                    terms.append((i, 0 if i == 0 else (1 if i == 7 else (2 if i % 2 == 0 else 3))))
                    if i < 7:
                        terms.append((i + 1, 6 if i % 2 == 0 else 7))
                    for ti, (j, kind) in enumerate(terms):
                        mm(ps[b][:, ii * 128:(ii + 1) * 128], uctok[:, j, g * 128:(g + 1) * 128], band(g, kind),
                           ti == 0, ti == len(terms) - 1, [("uctok", j), "cb"], [("ps", b)])
                tb = ntb()
                cpy(tmpb[tb][:], ps[b][:], [("ps", b)], [("tmpb", tb)], eng="act")
                b2 = nb()
                mm(ps[b2][:], wpool[:, g, :], tmpb[tb][:], True, True, ["wpool", ("tmpb", tb)], [("ps", b2)])
                act(oc[:, g, THS[th]], ps[b2][:], AF.Identity, [("ps", b2), "vecs"], [("oc", g, th)], scale=V("psc", idx=l * 4 + g))

        if DBG.get('stop') == 'pool':
            return
        obr_keys = [lambda kk, th: [("oa", th * 4 + q) for q in range(4)],
                    lambda kk, th: [("ob", th * 4 + q) for q in range(4)],
                    lambda kk, th: [("oc", kk, th)]]
        obr_src = [lambda kk, th: oa[:, kk, THS[th]], lambda kk, th: ob[:, kk, THS[th]], lambda kk, th: oc[:, kk, THS[th]]]
        for m in range(8):
            mc = slice(m * 128, (m + 1) * 128)
            slotg, wkg = wload([(lambda s, br=br: v3(s, 8, 384)[:, :, br * 128:(br + 1) * 128],
                                 wsrc(w_in, l, 2848 + br * 1024 + m * 128, 128)) for br in range(3)])
            pg = v3(slotg, 8, 384)
            slotb, wkb = wload([
                (lambda s: v3(s, 4, 384)[0:64, :, 0:128], w_br[0][l, 0:256, mc].rearrange("(j d) c -> d j c", d=64)),
                (lambda s: v3(s, 4, 384)[64:128, :, 0:128], w_br[0][l, 256:512, mc].rearrange("(j d) c -> d j c", d=64)),
                (lambda s: v3(s, 4, 384)[:, :, 128:256], w_br[1][l, :, mc].rearrange("(k p) c -> p k c", p=128)),
                (lambda s: v3(s, 4, 384)[:, :, 256:384], w_br[2][l, :, mc].rearrange("(k p) c -> p k c", p=128))])
            pb = v3(slotb, 4, 384)
            for th in range(2):
                tacc = ntf()
                for br in range(3):
                    bg = nb()
                    for k in range(8):
                        mm(ps[bg][:], pg[:, k, br * 128:(br + 1) * 128], hn[:, k, THS[th]], k == 0, k == 7, wkg + [("hn", k, th)], [("ps", bg)])
                    bbk = nb()
                    for kk in range(4):
                        mm(ps[bbk][:], pb[:, kk, br * 128:(br + 1) * 128], obr_src[br](kk, th), kk == 0, kk == 3,
                           wkb + obr_keys[br](kk, th), [("ps", bbk)])
                    tsg = ntf()
                    while tsg == tacc:
                        tsg = ntf()
                    act(tmpf[tsg][:], ps[bg][:], AF.Sigmoid, [("ps", bg)], [("tmpf", tsg)])
                    if br == 0:
                        tt(tmpf[tacc][:], tmpf[tsg][:], ps[bbk][:], ALU.mult, [("tmpf", tsg), ("ps", bbk)], [("tmpf", tacc)])
                    else:
                        tt(tmpf[tsg][:], tmpf[tsg][:], ps[bbk][:], ALU.mult, [("tmpf", tsg), ("ps", bbk)], [("tmpf", tsg)])
                        if br == 1:
                            tt(tmpf[tacc][:], tmpf[tacc][:], tmpf[tsg][:], ALU.add, [("tmpf", tsg), ("tmpf", tacc)], [("tmpf", tacc)])
                        else:
                            tt(mixed[:, m, THS[th]], tmpf[tacc][:], tmpf[tsg][:], ALU.add, [("tmpf", tsg), ("tmpf", tacc)], [("mixed", m, th)])

        if DBG.get('stop') == 'merge':
            return
        for half in range(2):
            slot, wk = wload([(lambda s: v3(s, 8, 512), wsrc(w_out, l, half * 512, 512))])
            pwo = v3(slot, 8, 512)
            for mm_ in range(4):
                m = half * 4 + mm_
                for th in range(2):
                    b = nb()
                    for k in range(8):
                        mm(ps[b][:], pwo[:, k, mm_ * 128:(mm_ + 1) * 128], mixed[:, k, THS[th]], k == 0, k == 7,
                           wk + [("mixed", k, th)], [("ps", b)])
                    stt(xT[:, m, THS[th]], ps[b][:], modT[:, 16 + m:17 + m], xT[:, m, THS[th]], ALU.mult, ALU.add,
                        [("ps", b), mk_, ("xT", m, th)], [("xT", m, th)])

        if DBG.get('stop') == 'outproj':
            return
        norm_mod(lambda k: A12[:, 8 + k:9 + k], lambda k: modT[:, 24 + k:25 + k], (ak_, 1), mk_,
                 filler=(lambda: mod_panels(l + 1, 0, 3)) if l + 1 < NL else None)
        for fc in range(11):
            slg, wkg = wload([(lambda s: v3(s, 8, 512)[:, :, 0:256], wsrc(w_fg, l, fc * 256, 256)),
                              (lambda s: v3(s, 8, 512)[:, :, 256:512], wsrc(w_fu, l, fc * 256, 256))])
            pgu = v3(slg, 8, 512)
            for ff in range(2):
                f = fc * 2 + ff
                for th in range(2):
                    bg = nb()
                    for k in range(8):
                        mm(ps[bg][:], pgu[:, k, ff * 128:(ff + 1) * 128], hn[:, k, THS[th]], k == 0, k == 7, wkg + [("hn", k, th)], [("ps", bg)])
                    bu = nb()
                    for k in range(8):
                        mm(ps[bu][:], pgu[:, k, 256 + ff * 128:256 + (ff + 1) * 128], hn[:, k, THS[th]], k == 0, k == 7, wkg + [("hn", k, th)], [("ps", bu)])
                    tsg = ntf()
                    act(tmpf[tsg][:], ps[bg][:], AF.Silu, [("ps", bg)], [("tmpf", tsg)])
                    tt(hT[:, f, THS[th]], tmpf[tsg][:], ps[bu][:], ALU.mult, [("tmpf", tsg), ("ps", bu)], [("hT", f, th)])
            if l + 1 < NL and fc == 0:
                mod_panels(l + 1, 3, 4)
        if l + 1 < NL:
            mod_finish(l + 1, part=1)
        for m in range(8):
            slot, wk = wload([(lambda s: v3(s, NFT, 128), w_fd[l, :, m * 128:(m + 1) * 128].rearrange("(k p) c -> p k c", p=128))])
            pd = v3(slot, NFT, 128)
            for th in range(2):
                b = nb()
                for f in range(NFT):
                    mm(ps[b][:], pd[:, f, :], hT[:, f, THS[th]], f == 0, f == NFT - 1, wk + [("hT", f, th)], [("ps", b)])
                stt(xT[:, m, THS[th]], ps[b][:], modT[:, 40 + m:41 + m], xT[:, m, THS[th]], ALU.mult, ALU.add,
                    [("ps", b), mk_, ("xT", m, th)], [("xT", m, th)])
            if l == NL - 1 and not DBG.get('skipy') and m >= 1:
                y_tile(m - 1)
        if l == NL - 1 and not DBG.get('skipy'):
            y_tile(7)

    def y_tile(m):
        st = stg[m % 4]
        sk = ("stg%d" % (m % 4),)
        for b4 in range(2):
            b = nb()
            for kk in range(4):
                blk = b4 * 4 + kk
                tr(ps[b][:, kk * 128:(kk + 1) * 128], xT[:, m, blk * 128:(blk + 1) * 128], [("xT", m, blk // 4)], [("ps", b)])
            cpy(st[:, b4 * 512:(b4 + 1) * 512], ps[b][:], [("ps", b)], [sk], eng=("act" if b4 == 0 else "dve"))
        dma("sp", y_d[:, m * 128:(m + 1) * 128].rearrange("(b p) c -> p b c", p=128), st[:].rearrange("p (b c) -> p b c", c=128),
            [sk], [], ("ystg%d" % (m % 4),))

    NL = DBG.get('L', L)
    if NL > 0:
        mod_panels(0, 0, 4)
        mod_finish(0, part=1)
    dma("pool", cb[:], cb_d, [], ["cb"], "cb")
    for l in range(NL):
        layer(l)

    tens = {"hn": hn, "QA": QA, "KA": KA, "kn": kn, "oa": oa, "QG": QG, "KG": KG, "Kp": Kp, "Vg": Vg, "ob": ob, "oc": oc,
            "mixed": mixed, "xT": xT, "modT": modTs[0], "S_all": S_all, "VP": VP, "KC": KC, "Gt": Gt, "aug2": aug2, "uctok": uctok}
    for (dname, dkeys) in DBG.get('dump', []):
        th_ = tens[dname]
        dd = nc.dram_tensor('dbg_' + dname, [int(v) for v in th_.shape], F32, kind='ExternalOutput').ap()
        dma('pool', dd, th_[:], ['*'], [], ('dbg', dname))

    for blk in range(0 if (DBG.get('skipy') or NL > 0) else NB):
        st = stg[blk % 4]
        for k4 in range(2):
            b = nb()
            for kk in range(4):
                k = k4 * 4 + kk
                tr(ps[b][:, kk * 128:(kk + 1) * 128], xT[:, k, blk * 128:(blk + 1) * 128], [("xT", k, blk // 4)], [("ps", b)])
            cpy(st[:, k4 * 512:(k4 + 1) * 512], ps[b][:], [("ps", b)], [("stg%d" % (blk % 4),)], eng=("act" if k4 == 0 else "dve"))
        dma("sp", y_d[blk * 128:(blk + 1) * 128, :], st[:], [("stg%d" % (blk % 4),)], [], ("ystg%d" % (blk % 4),))

    P.emit()
    return nc, total_sbuf


def _consts(role):
    s = np.arange(128)[:, None]
    t = np.arange(128)[None, :]
    same = (s // 64) == (t // 64)
    cfm = np.zeros((128, NCF), np.float32)
    cfm[:, CF_TT:CF_TT + 128] = same & (s <= t)
    cfm[:, CF_TT + 128:CF_TT + 256] = same & (s >= t)
    cfm[:, CF_SU:CF_SU + 128] = same & (s > t)
    cfm[:, CF_SL:CF_SL + 128] = same & (s < t)
    p = np.arange(128)
    dd = p % 64
    tok = np.arange(T)
    if role == "sample":
        i = dd % 16
        inv = (10000.0 ** (-(i.astype(np.float32)) / np.float32(16.0))).astype(np.float32)
        pos = np.where((dd < 32)[:, None], (tok // 64)[None, :], (tok % 64)[None, :]).astype(np.float32)
        ang = (pos * inv[:, None]).astype(np.float32)
        sgn = np.where((dd % 32) < 16, -1.0, 1.0)[:, None]
        ropec, ropes = np.cos(ang), np.sin(ang) * sgn
    else:
        ropec, ropes = np.ones((128, T), np.float32), np.zeros((128, T), np.float32)
    cbm = np.zeros((128, NCB), np.float32)
    cbm[:, CB_RC:CB_RC + T] = ropec
    cbm[:, CB_RS:CB_RS + T] = ropes
    partner = np.where((dd % 32) < 16, p + 16, p - 16)
    cbm[partner, CB_RM + p] = 1.0
    cbm[:, CB_MA:CB_MA + 128] = same & (s <= t)
    cbm[:, CB_MA + 128:CB_MA + 256] = same & (s >= t)
    NEGM = -30000.0
    if role == "sample":
        prev = np.where(s >= t, 0.0, NEGM)
        nxt = np.where(s <= t, 0.0, NEGM)
        am = [prev, prev, nxt, nxt]
    else:
        z = np.zeros((128, 128))
        n_ = np.full((128, 128), NEGM)
        am = [n_, z, z, n_]
    for k in range(4):
        cbm[:, CB_AM + k * 128:CB_AM + (k + 1) * 128] = am[k]
    Ls = 1024 if role == "sample" else 256
    for g, w in enumerate((2, 4, 8, 16)):
        Wf = np.zeros((T, T), np.float64)
        for tt_ in range(T):
            s0_ = (tt_ // Ls) * Ls
            tl = tt_ - s0_
            lo = min(max(tl - w // 2, 0), Ls)
            hi = min(max(tl - w // 2 + w, 0), Ls)
            Wf[s0_ + lo:s0_ + hi, tt_] = 1.0 / (hi - lo)
            Wf[tt_, tt_] -= 1.0
        blkm = lambda j, i: Wf[j * 128:(j + 1) * 128, i * 128:(i + 1) * 128]
        kinds = [blkm(0, 0), blkm(7, 7), blkm(2, 2), blkm(1, 1), blkm(0, 1), blkm(1, 2), blkm(1, 0), blkm(2, 1)]
        for i in (2, 4, 6):
            assert np.array_equal(blkm(i, i), kinds[2]) and np.array_equal(blkm(i - 1, i), kinds[5]) and np.array_equal(blkm(i + 1, i), kinds[6])
        for i in (1, 3, 5):
            assert np.array_equal(blkm(i, i), kinds[3]) and np.array_equal(blkm(i - 1, i), kinds[4]) and np.array_equal(blkm(i + 1, i), kinds[7])
        assert np.array_equal(blkm(1, 0), kinds[6]) and np.array_equal(blkm(6, 7), kinds[4])
        for kind in range(8):
            o = CB_BAND + (g * 8 + kind) * 128
            cbm[:, o:o + 128] = kinds[kind]
    return cfm, cbm


def kernel(x_prompt, x_sample, c, cache_k, cache_v, state_gla, c_ctx, w_in, g_qn, g_kn, att_sink,
           w_gate2, b_gate2, g_gla_out, w_pool, pool_scale, w_br_a, w_br_b, w_br_c, w_out,
           g_norm1, g_norm2, w_mod, b_mod, w_ff_gate, w_ff_up, w_ff_down):
    f = lambda a: np.ascontiguousarray(np.asarray(a, dtype=np.float32))
    x_prompt, x_sample, c, cache_k, cache_v, state_gla, c_ctx = map(f, (x_prompt, x_sample, c, cache_k, cache_v, state_gla, c_ctx))
    g_qn, g_kn, att_sink, w_gate2, b_gate2, g_gla_out, pool_scale = map(f, (g_qn, g_kn, att_sink, w_gate2, b_gate2, g_gla_out, pool_scale))
    g_norm1, g_norm2, b_mod = map(f, (g_norm1, g_norm2, b_mod))
    shared = {"w_in": f(w_in), "w_pool": f(w_pool), "w_br_a": f(w_br_a), "w_br_b": f(w_br_b), "w_br_c": f(w_br_c),
              "w_out": f(w_out), "w_mod": f(w_mod), "w_ff_gate": f(w_ff_gate), "w_ff_up": f(w_ff_up), "w_ff_down": f(w_ff_down)}
    w2 = np.zeros((64, L, 4, 2, 64), np.float32)
    for l in range(L):
        for r in range(2):
            w2[r * 32:r * 32 + 16, l, :, r, :] = w_gate2[l, r].reshape(16, 4, 64)
            w2[r * 32 + 16, l, :, r, :] = b_gate2[l, r].reshape(4, 64)
    shared["w2aug"] = np.ascontiguousarray(w2.reshape(64, L * 512))
    p = np.arange(128)

    def vec_base(cv, ctxb, mres):
        v = np.zeros((128, NVEC), np.float32)
        v[:, VEC_OFF["gn1"]:VEC_OFF["gn1"] + 32] = g_norm1.reshape(L, 8, 128).transpose(2, 0, 1).reshape(128, 32)
        v[:, VEC_OFF["gn2"]:VEC_OFF["gn2"] + 32] = g_norm2.reshape(L, 8, 128).transpose(2, 0, 1).reshape(128, 32)
        v[:, VEC_OFF["gq"]:VEC_OFF["gq"] + 4] = g_qn[:, p % 64].T
        v[:, VEC_OFF["gk"]:VEC_OFF["gk"] + 4] = g_kn[:, p % 64].T
        v[:, VEC_OFF["ggla"]:VEC_OFF["ggla"] + 4] = g_gla_out.T
        v[:, VEC_OFF["psc"]:VEC_OFF["psc"] + 16] = pool_scale.reshape(L, 4, 128).transpose(2, 0, 1).reshape(128, 16)
        sk = att_sink.reshape(L, 2, 4)
        v[:, VEC_OFF["sink"]:VEC_OFF["sink"] + 16] = sk[:, p // 64, :].transpose(1, 0, 2).reshape(128, 16)
        v[:, VEC_OFF["cvec"]:VEC_OFF["cvec"] + 8] = cv.reshape(8, 128).T
        v[:, VEC_OFF["ctxb"]] = ctxb
        v[:, VEC_OFF["mres"]] = mres
        v[:, VEC_OFF["bmod"]:VEC_OFF["bmod"] + 192] = b_mod.reshape(L, 48, 128).transpose(2, 0, 1).reshape(128, 192)
        return v

    cons = {"prompt": _consts("prompt"), "sample": _consts("sample")}
    in_maps = []
    for core in range(8):
        m = dict(shared)
        if core < 4:
            m["x_in"] = np.ascontiguousarray(x_prompt[core * 4:(core + 1) * 4].reshape(T, D))
            m["vecs"] = vec_base(c_ctx, -1e30, 0.0)
            m["cf"], m["cb"] = cons["prompt"]
            m["ctxk"] = np.zeros((L, 256, 128), np.float32)
            m["ctxv"] = np.zeros((L, 256, 128), np.float32)
            m["s0"] = np.zeros((L, 2, 4, 64, 128), np.float32)
        else:
            bi = (core - 4) % 2
            m["x_in"] = np.ascontiguousarray(x_sample[bi])
            m["vecs"] = vec_base(c[bi], 0.0, 1.0)
            m["cf"], m["cb"] = cons["sample"]
            m["ctxk"] = np.ascontiguousarray(cache_k[bi].reshape(L, 256, 128))
            m["ctxv"] = np.ascontiguousarray(cache_v[bi].reshape(L, 256, 128))
            m["s0"] = np.ascontiguousarray(state_gla[bi])
        in_maps.append(m)
    nc, _ = build_program()
    res = run_bass_kernel_spmd(nc, in_maps, core_ids=list(range(8)))
    r = res.results
    y_prompt = np.concatenate([r[cidx]["y"].reshape(4, 256, D) for cidx in range(4)], axis=0)
    y_sample = np.stack([r[4]["y"], r[5]["y"]], axis=0)
    nk = np.concatenate([r[cidx]["nk"].reshape(L, 4, 256, 2, 64).transpose(1, 0, 2, 3, 4) for cidx in range(4)], axis=0)
    nv = np.concatenate([r[cidx]["nv"].reshape(L, 4, 256, 2, 64).transpose(1, 0, 2, 3, 4) for cidx in range(4)], axis=0)
    nst = np.concatenate([r[cidx]["nst"].transpose(1, 0, 2, 3, 4, 5) for cidx in range(4)], axis=0)
    return (np.ascontiguousarray(y_prompt, dtype=np.float32), np.ascontiguousarray(y_sample, dtype=np.float32),
            np.ascontiguousarray(nk, dtype=np.float32), np.ascontiguousarray(nv, dtype=np.float32),
            np.ascontiguousarray(nst, dtype=np.float32))
```

```python
import contextlib
import numpy as np
import concourse.bass as bass
import concourse.mybir as mybir
from concourse.bass_utils import run_bass_kernel_spmd

F32 = mybir.dt.float32
BF16 = mybir.dt.bfloat16
AF = mybir.ActivationFunctionType
ALU = mybir.AluOpType

ENGS = ("pe", "act", "dve", "pool", "sp")

T = 1024
NB = 8
D = 1024
L = 4
DFF = 2816
NFT = 22
EPS = 1e-6
W_IN = 5920
NSLOT = 4
NTF = 8
NTB = 6
DBG = {}
SLOTW = 4480


class Op:
    __slots__ = ("eng", "fn", "deps", "signal", "idx", "dma_slot", "dma_val", "waits")

    def __init__(self, eng, fn):
        self.eng = eng
        self.fn = fn
        self.deps = set()
        self.signal = False
        self.idx = 0
        self.dma_slot = None
        self.dma_val = 0
        self.waits = []


class Prog:
    def __init__(self, nc):
        self.nc = nc
        self.ops = []
        self.by_eng = {e: [] for e in ENGS}
        self.last_w = {}
        self.readers = {}
        self.dma_slots = {}
        self.alias_owner = {}
        self.alias_flip = {}
        self.alias_of = {}
        self.keys_of = {}

    def alias(self, group, *names):
        for n in names:
            self.alias_of.setdefault(n, []).append(group)

    def add(self, eng, fn, reads=(), writes=(), dma_slot=None):
        op = Op(eng, fn)
        reads = list(reads)
        writes = list(writes)
        psr = [k for k in reads if isinstance(k, tuple) and k[0] == "ps"]
        if psr:
            reads = [k for k in reads if k not in psr]
            writes = writes + [k for k in psr if k not in writes]
        for k in reads + writes:
            n = k[0] if isinstance(k, tuple) else k
            for g in self.alias_of.get(n, ()):
                own = self.alias_owner.get(g)
                if own != n:
                    fd = set()
                    if own is not None:
                        for ok in self.keys_of.get(own, ()):
                            w = self.last_w.get(ok)
                            if w is not None:
                                fd.add(w)
                            for r in self.readers.get(ok, ()):
                                fd.add(r)
                    self.alias_flip[g] = fd
                    self.alias_owner[g] = n
                op.deps |= self.alias_flip.get(g, set())
            if n in self.alias_of:
                self.keys_of.setdefault(n, set()).add(k)
        for k in reads:
            w = self.last_w.get(k)
            if w is not None:
                op.deps.add(w)
        for k in writes:
            w = self.last_w.get(k)
            if w is not None:
                op.deps.add(w)
            for r in self.readers.get(k, ()):
                op.deps.add(r)
        for k in reads:
            self.readers.setdefault(k, []).append(op)
        for k in writes:
            self.last_w[k] = op
            self.readers[k] = []
        if "*" in reads:
            for e in ENGS:
                if self.by_eng[e]:
                    op.deps.add(self.by_eng[e][-1])
            for o2 in self.ops:
                if o2.dma_slot is not None:
                    op.deps.add(o2)
        op.deps.discard(op)
        if dma_slot is not None:
            op.dma_slot = dma_slot
            c = self.dma_slots.get(dma_slot, 0) + 1
            self.dma_slots[dma_slot] = c
            op.dma_val = 16 * c
        self.ops.append(op)
        self.by_eng[eng].append(op)
        return op

    def emit(self):
        nc = self.nc
        for op in self.ops:
            for d in op.deps:
                if d.dma_slot is not None:
                    continue
                if d.eng == op.eng and op.eng == "pe" and op.dma_slot is None:
                    continue
                d.signal = True
        cnt = {e: 0 for e in ENGS}
        for op in self.ops:
            if op.dma_slot is None and op.signal:
                cnt[op.eng] += 1
                op.idx = cnt[op.eng]
        final_cnt = dict(cnt)
        waited = {e: {} for e in ENGS}
        for op in self.ops:
            need = {}
            for d in op.deps:
                if d.dma_slot is not None:
                    key = ("dma", d.dma_slot)
                    need[key] = max(need.get(key, 0), d.dma_val)
                else:
                    if not d.signal:
                        continue
                    if d.eng == op.eng and op.eng == "pe" and op.dma_slot is None:
                        continue
                    key = ("eng", d.eng)
                    need[key] = max(need.get(key, 0), d.idx)
            w = waited[op.eng]
            for key, v in need.items():
                if w.get(key, 0) >= v:
                    continue
                w[key] = v
                op.waits.append((key, v))
        if DBG.get('sim'):
            sem = {}
            pos = {e: 0 for e in ENGS}
            prog = True
            while prog:
                prog = False
                for e in ENGS:
                    while pos[e] < len(self.by_eng[e]):
                        op = self.by_eng[e][pos[e]]
                        if all(sem.get(k, 0) >= v for k, v in op.waits):
                            if op.dma_slot is not None:
                                sem[('dma', op.dma_slot)] = sem.get(('dma', op.dma_slot), 0) + 16
                            elif op.signal:
                                sem[('eng', e)] = sem.get(('eng', e), 0) + 1
                                assert sem[('eng', e)] == op.idx
                            pos[e] += 1
                            prog = True
                        else:
                            break
            print('SIM done:', {e: (pos[e], len(self.by_eng[e])) for e in ENGS})
            for e in ENGS:
                if pos[e] < len(self.by_eng[e]):
                    op = self.by_eng[e][pos[e]]
                    print('STUCK', e, pos[e], op.waits, {k: sem.get(k, 0) for k, v in op.waits})
        if DBG.get('print'):
            for e in ENGS:
                print('ENGINE', e)
                for op in self.by_eng[e]:
                    print('   ', op.waits, 'sig' if op.signal else '', op.idx, op.dma_slot, op.dma_val)
        with contextlib.ExitStack() as es:
            esem = {e: es.enter_context(nc.semaphore("c_" + e)) for e in ENGS}
            dsem = {s: es.enter_context(nc.semaphore("d_%d" % i)) for i, s in enumerate(self.dma_slots)}
            block = es.enter_context(nc.Block())

            def run(engname, eng):
                for op in self.by_eng[engname]:
                    for key, v in op.waits:
                        if key[0] == "dma":
                            eng.wait_ge(dsem[key[1]], v)
                        else:
                            eng.wait_ge(esem[key[1]], v)
                    ins = op.fn(eng)
                    if op.dma_slot is not None:
                        ins.then_inc(dsem[op.dma_slot], 16)
                    elif op.signal:
                        ins.then_inc(esem[engname], 1)
                if engname == "sp":
                    for s, c in self.dma_slots.items():
                        eng.wait_ge(dsem[s], 16 * c)
                    for e in ENGS:
                        if e != "sp" and final_cnt[e] > 0:
                            eng.wait_ge(esem[e], final_cnt[e])

            block.tensor(lambda e: run("pe", e))
            block.scalar(lambda e: run("act", e))
            block.vector(lambda e: run("dve", e))
            block.gpsimd(lambda e: run("pool", e))
            block.sync(lambda e: run("sp", e))


VEC_OFF = {}
_o = 0
for _n, _w in (("gn1", 32), ("gn2", 32), ("gq", 4), ("gk", 4), ("ggla", 4), ("psc", 16), ("sink", 16),
               ("cvec", 8), ("ctxb", 1), ("mres", 1), ("bmod", 192)):
    VEC_OFF[_n] = _o
    _o += _w
NVEC = _o

CF_TT, CF_SU, CF_SL, NCF = 0, 256, 384, 512
CB_RM, CB_MA, CB_AM, CB_BAND, CB_RC, CB_RS, NCB = 0, 128, 384, 896, 896 + 4096, 896 + 4096 + 1024, 896 + 4096 + 2048


def build_program():
    nc = bass.Bass("TRN2", target_bir_lowering=False)
    dt = lambda name, shape, kind="ExternalInput": nc.dram_tensor(name, list(shape), F32, kind=kind).ap()
    x_in = dt("x_in", [T, D])
    vecs_d = dt("vecs", [128, NVEC])
    cf_d = dt("cf", [128, NCF])
    cb_d = dt("cb", [128, NCB])
    w2_d = dt("w2aug", [64, L * 512])
    ctxk_d = dt("ctxk", [L, 256, 128])
    ctxv_d = dt("ctxv", [L, 256, 128])
    s0_d = dt("s0", [L, 2, 4, 64, 128])
    w_in = dt("w_in", [L, D, W_IN])
    w_pool_d = dt("w_pool", [L, 4, 128, 128])
    w_br = [dt("w_br_a", [L, 512, D]), dt("w_br_b", [L, 512, D]), dt("w_br_c", [L, 512, D])]
    w_out = dt("w_out", [L, D, D])
    w_mod = dt("w_mod", [L, D, 6 * D])
    w_fg = dt("w_ff_gate", [L, D, DFF])
    w_fu = dt("w_ff_up", [L, D, DFF])
    w_fd = dt("w_ff_down", [L, DFF, D])
    y_d = dt("y", [T, D], "ExternalOutput")
    nk_d = dt("nk", [L, T, 128], "ExternalOutput")
    nv_d = dt("nv", [L, T, 128], "ExternalOutput")
    nst_d = dt("nst", [L, 4, 2, 4, 64, 128], "ExternalOutput")

    off = [(nc.sbuf_base + 63) // 64 * 64]

    def alloc(name, shape, dtype, at=None):
        nbytes = int(np.prod(shape[1:])) * (4 if dtype == F32 else 2)
        nbytes = (nbytes + 31) // 32 * 32
        if at is None:
            o = off[0]
            off[0] += nbytes
        else:
            o = at
        return nc.alloc_sbuf_tensor_at(name, list(shape), dtype, offset=o)

    xT = alloc("xT", [128, 8, T], F32)
    hn = alloc("hn", [128, 8, T], BF16)
    ring = [alloc("ring%d" % i, [128, SLOTW], BF16) for i in range(NSLOT)]
    vecs = alloc("vecs_s", [128, NVEC], F32)
    cf = alloc("cf_s", [128, NCF], F32)
    cb = alloc("cb_s", [128, NCB], BF16)
    w2all = alloc("w2all", [64, 512], BF16)
    wpool = alloc("wpool", [128, 4, 128], BF16)
    ident = alloc("ident", [128, 128], F32)
    identb = alloc("identb", [128, 128], BF16)
    onesb = alloc("onesb", [128, 128], BF16)
    bones = alloc("bones", [128, 128], BF16)
    ones01 = alloc("ones01", [128, 2, 128], BF16)
    onef = alloc("onef", [1, 8], F32)
    aug2 = alloc("aug2", [64, T], BF16)
    scT = alloc("scT", [128, 8], BF16)
    modTs = [alloc("modT%d" % i, [128, 48], F32) for i in range(2)]
    modrow = alloc("modrow", [1, 512], F32)
    A12s = [alloc("A12_%d" % i, [128, 16], F32) for i in range(2)]
    esink = alloc("esink", [128, 16], F32)
    Gt = alloc("Gt", [128, 4, 16], F32)
    Sst = alloc("Sst", [128, 4, 128], F32)
    tmpf = [alloc("tmpf%d" % i, [128, 512], F32) for i in range(NTF)]
    tmpb = [alloc("tmpb%d" % i, [128, 512], BF16) for i in range(NTB)]
    a0 = off[0]
    QA = alloc("QA", [128, 4, T], BF16)
    KA = alloc("KA", [128, 2, T], BF16)
    KC = alloc("KC", [128, 2, 256], BF16)
    VP = alloc("VP", [128, 8, 2, 128], BF16)
    VPC = alloc("VPC", [128, 2, 2, 128], BF16)
    assert off[0] - a0 <= 18432
    off[0] = a0 + 18432
    c2 = off[0]
    kn = alloc("kn", [128, T], F32)
    oa = alloc("oa", [128, 4, T], BF16)
    c4 = off[0]
    QG = alloc("QG", [128, 4, T], BF16)
    KG = alloc("KG", [128, 4, T], BF16)
    a_end_ht = off[0]
    c6 = off[0]
    kbtok = alloc("kbtok", [128, 8, 256], BF16)
    c7 = off[0]
    Vg = alloc("Vg", [128, 8, 512], BF16)
    c8 = off[0]
    Kp = alloc("Kp", [128, 8, 4, 2, 64], BF16)
    c9 = off[0]
    ob = alloc("ob", [128, 4, T], BF16)
    S_all = alloc("S_all", [128, 16, 4, 128], BF16, at=a0)
    oc = alloc("oc", [128, 4, T], BF16, at=a0)
    uctok = alloc("uctok", [128, 8, 512], BF16, at=c8)
    stSs = [alloc("stSa", [128, 2, 4, 128], F32, at=c2), alloc("stSb", [128, 2, 4, 128], F32, at=c6)]
    stg = [alloc("stg0", [128, 1024], F32, at=c9), alloc("stg1", [128, 1024], F32, at=c7),
           alloc("stg2", [128, 1024], F32, at=c8), alloc("stg3", [128, 1024], F32, at=c9 + 4096)]
    cst = alloc("cst", [128, 2, 2, 128], F32, at=c6)
    mixed = alloc("mixed", [128, 8, T], BF16, at=c4)
    hT = alloc("hT", [128, NFT, T], BF16, at=a0)
    assert a_end_ht - a0 >= NFT * T * 2, (a_end_ht - a0)
    total_sbuf = off[0]
    ps = [nc.alloc_psum_tensor("ps%d" % i, [128, 512], F32) for i in range(8)]

    P = Prog(nc)
    P.alias("r0a", "QA", "S_all", "oc", "hT")
    P.alias("r0b", "KA", "S_all", "hT")
    P.alias("r0c", "KC", "S_all", "hT")
    P.alias("r0d", "VP", "S_all", "hT")
    P.alias("r0e", "VPC", "S_all", "hT")
    P.alias("r2", "kn", "stSa", "hT")
    P.alias("r3", "oa", "hT")
    P.alias("r4", "QG", "mixed", "hT")
    P.alias("r5", "KG", "mixed", "hT")
    P.alias("r6", "kbtok", "cst", "stSb")
    P.alias("r7", "Vg", "stg1")
    P.alias("r8", "Kp", "uctok", "stg2")
    P.alias("r9", "ob", "stg0", "stg3")

    bank_rr = [0]

    def nb():
        b = bank_rr[0]
        bank_rr[0] = (b + 1) % 8
        return b

    tf_rr = [0]
    tb_rr = [0]

    def ntf():
        i = tf_rr[0]
        tf_rr[0] = (i + 1) % NTF
        return i

    def ntb():
        i = tb_rr[0]
        tb_rr[0] = (i + 1) % NTB
        return i

    def mm(out, lhsT, rhs, start, stop, reads, writes):
        P.add("pe", lambda e: e.matmul(out, lhsT=lhsT, rhs=rhs, start=start, stop=stop), reads=reads, writes=writes)

    def tr(out, in_, reads, writes):
        P.add("pe", lambda e: e.transpose(out=out, in_=in_, identity=ident[:]), reads=list(reads) + ["ident"], writes=writes)

    def act(out, in_, func, reads, writes, bias=None, scale=None):
        kw = {}
        if bias is not None:
            kw["bias"] = bias
        if scale is not None:
            kw["scale"] = scale
        P.add("act", lambda e: e.activation(out=out, in_=in_, func=func, **kw), reads=reads, writes=writes)

    def tt(out, in0, in1, op, reads, writes, eng="dve"):
        P.add(eng, lambda e: e.tensor_tensor(out=out, in0=in0, in1=in1, op=op), reads=reads, writes=writes)

    def ts(out, in0, s1, s2, op0, op1, reads, writes):
        if s2 is None:
            P.add("dve", lambda e: e.tensor_scalar(out=out, in0=in0, scalar1=s1, scalar2=None, op0=op0), reads=reads, writes=writes)
        else:
            P.add("dve", lambda e: e.tensor_scalar(out=out, in0=in0, scalar1=s1, scalar2=s2, op0=op0, op1=op1), reads=reads, writes=writes)

    def stt(out, in0, scalar, in1, op0, op1, reads, writes):
        P.add("dve", lambda e: e.scalar_tensor_tensor(out=out, in0=in0, scalar=scalar, in1=in1, op0=op0, op1=op1), reads=reads, writes=writes)

    def cpy(out, in_, reads, writes, eng="dve"):
        if eng == "act":
            act(out, in_, AF.Copy, reads, writes)
        else:
            P.add("dve", lambda e: e.tensor_copy(out=out, in_=in_), reads=reads, writes=writes)

    def rcp(out, in_, reads, writes):
        P.add("dve", lambda e: e.reciprocal(out=out, in_=in_), reads=reads, writes=writes)

    def mset(ap, val, writes, eng="dve"):
        P.add(eng, lambda e: e.memset(ap, val), writes=writes)

    def dma(eng, out, in_, reads, writes, slot):
        P.add(eng, lambda e: e.dma_start(out=out, in_=in_), reads=reads, writes=writes, dma_slot=slot)

    ring_rr = [0]

    def wload(parts):
        s_ = ring_rr[0]
        ring_rr[0] = (s_ + 1) % NSLOT
        MAXP = 11
        assert len(parts) <= MAXP
        keys = [("w", s_, pi) for pi in range(MAXP)]
        for pi, (dfn, src) in enumerate(parts):
            wr = [("w", s_, pi)]
            if pi == 0:
                wr += [("w", s_, q) for q in range(len(parts), MAXP)]
            dma("pool", dfn(ring[s_]), src, [], wr, ("w", s_, pi))
        return ring[s_], keys

    def V(name, l=None, width=1, idx=0):
        o = VEC_OFF[name] + idx
        return vecs[:, o:o + width]

    dma("sp", vecs[:], vecs_d, [], ["vecs"], "vecs")
    dma("sp", cf[:], cf_d, [], ["cf"], "cf")
    P.add("pool", lambda e: e.memset(ident[:], 1.0), writes=["ident"])
    P.add("pool", lambda e: e.affine_select(out=ident[:], in_=ident[:], pattern=[[-1, 128]], compare_op=ALU.is_equal,
                                            fill=0.0, base=0, channel_multiplier=1), reads=["ident"], writes=["ident"])
    cpy(identb[:], ident[:], ["ident"], ["identb"])
    mset(onesb[:], 1.0, ["onesb"])
    mset(bones[:], 0.0, ["bones"])
    mset(bones[0:64, 0:64], 1.0, ["bones"])
    mset(bones[64:128, 64:128], 1.0, ["bones"])
    mset(ones01[:], 0.0, ["ones01"])
    mset(ones01[:, 0, 0:64], 1.0, ["ones01"])
    mset(ones01[:, 1, 64:128], 1.0, ["ones01"])
    mset(onef[:], 1.0, ["onef"])
    mset(aug2[:], 1.0, ["aug2"])
    act(scT[:], V("cvec", width=8), AF.Silu, ["vecs"], ["scT"])
    act(esink[:], V("sink", width=16), AF.Exp, ["vecs"], ["esink"])

    for blk in range(0 if DBG.get('skipx') else NB):
        st = stg[blk % 4]
        dma("sp", st[:], x_in[blk * 128:(blk + 1) * 128, :], [], [("stg%d" % (blk % 4),)], ("stg%d" % (blk % 4),))
        for k4 in range(2):
            b = nb()
            for kk in range(4):
                k = k4 * 4 + kk
                tr(ps[b][:, kk * 128:(kk + 1) * 128], st[:, k * 128:(k + 1) * 128], [("stg%d" % (blk % 4),)], [("ps", b)])
            cpy(xT[:, k4 * 4:(k4 + 1) * 4, blk * 128:(blk + 1) * 128], ps[b][:].rearrange("p (a t) -> p a t", a=4),
                [("ps", b)], [("xT", k4 * 4 + kk, blk // 4) for kk in range(4)], eng=("act" if k4 == 0 else "dve"))

    THS = [slice(0, 512), slice(512, 1024)]

    def norm_mod(acol, shcol, ak_, mk_, filler=None):
        tfrs = []
        for th in range(2):
            b = nb()
            for k in range(8):
                tb = ntb()
                if k % 4 != 3:
                    tt(tmpb[tb][:], xT[:, k, THS[th]], xT[:, k, THS[th]], ALU.mult, [("xT", k, th)], [("tmpb", tb)])
                else:
                    act(tmpb[tb][:], xT[:, k, THS[th]], AF.Square, [("xT", k, th)], [("tmpb", tb)])
                mm(ps[b][:], onesb[:], tmpb[tb][:], k == 0, k == 7, [("tmpb", tb), "onesb"], [("ps", b)])
            tfr = ntf()
            act(tmpf[tfr][:], ps[b][:], AF.Ln, [("ps", b)], [("tmpf", tfr)], bias=EPS, scale=1.0 / D)
            act(tmpf[tfr][:], tmpf[tfr][:], AF.Exp, [("tmpf", tfr)], [("tmpf", tfr)], scale=-0.5)
            tfrs.append(tfr)
        if filler is not None:
            filler()
        for th in range(2):
            tfr = tfrs[th]
            for k in range(8):
                tf2 = ntf()
                while tf2 in tfrs:
                    tf2 = ntf()
                stt(tmpf[tf2][:], xT[:, k, THS[th]], acol(k), tmpf[tfr][:], ALU.mult, ALU.mult,
                    [("xT", k, th), ("tmpf", tfr), ak_, mk_], [("tmpf", tf2)])
                act(hn[:, k, THS[th]], tmpf[tf2][:], AF.Identity, [("tmpf", tf2), mk_], [("hn", k, th)], bias=shcol(k), scale=1.0)

    def proj_fm(lhs_fn, wkeys, cb_fn, kt=8, rhs_src=None, rhs_keys=None):
        for th in range(2):
            b = nb()
            for k in range(kt):
                if rhs_src is None:
                    r, rk = hn[:, k, THS[th]], [("hn", k, th)]
                else:
                    r, rk = rhs_src(k, th), rhs_keys(k, th)
                mm(ps_out(b, lhs_fn(k)), lhs_fn(k), r, k == 0, k == kt - 1, list(wkeys) + rk, [("ps", b)])
            cb_fn(b, th)

    def ps_out(b, lhs):
        m = lhs.shape[-1]
        return ps[b][0:m, :]

    def proj_tm(rhs_fn, n, wkeys, cb_fn):
        for blk in range(NB):
            b = nb()
            for k in range(8):
                mm(ps[b][:, 0:n], hn[:, k, blk * 128:(blk + 1) * 128], rhs_fn(k), k == 0, k == 7,
                   list(wkeys) + [("hn", k, blk // 4)], [("ps", b)])
            cb_fn(b, blk)

    def wsrc(w, l, c0, n):
        return w[l, :, c0:c0 + n].rearrange("(k p) c -> p k c", p=128)

    def v3(slot, kt, c):
        return slot[:, 0:kt * c].rearrange("p (k c) -> p k c", c=c)

    def mod_panels(l, n0, n1):
        modT = modTs[l % 2]
        mk = ("modT", l % 2)
        for n in range(n0, n1):
            slot, wk = wload([(lambda s: v3(s, 8, 512), wsrc(w_mod, l, n * 512, 512))])
            pw = v3(slot, 8, 512)
            b = nb()
            for k in range(8):
                mm(ps[b][0:1, :], scT[:, k:k + 1], pw[:, k, :], k == 0, k == 7, wk + ["scT"], [("ps", b)])
            cpy(modrow[:], ps[b][0:1, :], [("ps", b)], ["modrow"], eng="act")
            b2 = nb()
            for j in range(4):
                mm(ps[b2][:, j:j + 1], modrow[0:1, j * 128:(j + 1) * 128], onef[0:1, 0:1], True, True, ["modrow", "onef"], [("ps", b2)])
            tt(modT[:, n * 4:(n + 1) * 4], ps[b2][:, 0:4], V("bmod", width=4, idx=l * 48 + n * 4), ALU.add,
               [("ps", b2), "vecs"], [mk])

    def mod_finish(l, part=3):
        modT = modTs[l % 2]
        A12 = A12s[l % 2]
        mk = ("modT", l % 2)
        ak = ("A12", l % 2)
        if part & 1:
            ts(A12[:, 0:8], modT[:, 8:16], 1.0, None, ALU.add, None, [mk], [(ak, 0)])
            tt(A12[:, 0:8], A12[:, 0:8], V("gn1", width=8, idx=l * 8), ALU.mult, [(ak, 0), "vecs"], [(ak, 0)])
        if part & 2:
            ts(A12[:, 8:16], modT[:, 32:40], 1.0, None, ALU.add, None, [mk], [(ak, 1)])
            tt(A12[:, 8:16], A12[:, 8:16], V("gn2", width=8, idx=l * 8), ALU.mult, [(ak, 1), "vecs"], [(ak, 1)])

    def layer(l):
        dma("pool", w2all[:], w2_d[:, l * 512:(l + 1) * 512], [], ["w2all"], "w2all")
        dma("pool", wpool[:], w_pool_d[l].rearrange("g i o -> i g o"), [], ["wpool"], "wpool")
        modT = modTs[l % 2]
        A12 = A12s[l % 2]
        mk_ = ("modT", l % 2)
        ak_ = ("A12", l % 2)
        if DBG.get('stop') == 'mod':
            return
        norm_mod(lambda k: A12[:, k:k + 1], lambda k: modT[:, k:k + 1], (ak_, 0), mk_)

        if DBG.get('stop') == 'norm1':
            return
        cm = DBG.get('ctxmask', 15)
        if cm & 1:
            dma("sp", cst[:, 0, :, :], ctxk_d[l].rearrange("(b p) c -> p b c", p=128), [], [("cst", 0)], ("cst", 0))
            dma("sp", cst[:, 1, :, :], ctxv_d[l].rearrange("(b p) c -> p b c", p=128), [], [("cst", 1)], ("cst", 1))
        if cm & 2:
            mset(VP[:, :, 0, 64:128], 0.0, [("VP", "z0")])
            mset(VP[:, :, 1, 0:64], 0.0, [("VP", "z1")])
            mset(VPC[:, :, 0, 64:128], 0.0, [("VPC", "z0")])
            mset(VPC[:, :, 1, 0:64], 0.0, [("VPC", "z1")])
            mset(KA[64:128, 0, :], 0.0, [("KA", "z0")])
            mset(KA[0:64, 1, :], 0.0, [("KA", "z1")])
            mset(KC[64:128, 0, :], 0.0, [("KC", "z0")])
            mset(KC[0:64, 1, :], 0.0, [("KC", "z1")])
        if cm & 4:
            b = nb()
            for cbk in range(2):
                tr(ps[b][:, cbk * 128:(cbk + 1) * 128], cst[:, 0, cbk, :], [("cst", 0)], [("ps", b)])
            cpy(KC[0:64, 0, :], ps[b][0:64, 0:256], [("ps", b)], [("KC", "v0")], eng="act")
            cpy(KC[64:128, 1, :], ps[b][64:128, 0:256], [("ps", b)], [("KC", "v1")], eng="act")
        if cm & 8:
            cpy(VPC[:, :, 0, 0:64], cst[:, 1, :, 0:64], [("cst", 1)], [("VPC", "v0")])
            cpy(VPC[:, :, 1, 64:128], cst[:, 1, :, 64:128], [("cst", 1)], [("VPC", "v1")])
        if DBG.get('stop') == 'ctx':
            return
        slot, wk0 = wload([
            (lambda s, g=g, j=j: s[:, 0:8 * 512].rearrange("p (k j g d) -> p k j g d", k=8, j=4, g=2)[:, :, j, g, :],
             w_in[l, :, g * 256 + j * 64:g * 256 + (j + 1) * 64].rearrange("(k p) d -> p k d", p=128)) for g in range(2) for j in range(4)])
        pw = v3(slot, 8, 512)
        slot, wk = wload([(lambda s: v3(s, 8, 512)[:, :, 0:256], wsrc(w_in, l, 512, 256)),
                          (lambda s: v3(s, 8, 512)[:, :, 256:512], wsrc(w_in, l, 1024, 256))])
        pw1 = v3(slot, 8, 512)
        qk_items = []
        for j in range(4):
            for th in range(2):
                qk_items.append((lambda k, j=j: pw[:, k, j * 128:(j + 1) * 128], wk0, th, QA[:, j, THS[th]], [("QA", j, th)], V("gq", idx=l), False))
        for th in range(2):
            qk_items.append((lambda k: pw1[:, k, 0:128], wk, th, None, [("KA", th, 0), ("KA", th, 1)], V("gk", idx=l), True))
        qst = {}

        def qkA(i):
            lhs_fn, wkeys, th, dst, dkey, gcol, keep = qk_items[i]
            b_ = nb()
            for k in range(8):
                mm(ps[b_][:], lhs_fn(k), hn[:, k, THS[th]], k == 0, k == 7, list(wkeys) + [("hn", k, th)], [("ps", b_)])
            tb = i % 2
            act(tmpb[tb][:], ps[b_][:], AF.Square, [("ps", b_)], [("tmpb", tb)])
            qst[i] = b_

        def qkB(i):
            lhs_fn, wkeys, th, dst, dkey, gcol, keep = qk_items[i]
            b_ = qst[i]
            tb = i % 2
            base = 4 * (i % 2)
            b2 = nb()
            mm(ps[b2][:], bones[:], tmpb[tb][:], True, True, [("tmpb", tb), "bones"], [("ps", b2)])
            t1 = base
            act(tmpf[t1][:], ps[b2][:], AF.Ln, [("ps", b2)], [("tmpf", t1)], bias=EPS, scale=1.0 / 64)
            act(tmpf[t1][:], tmpf[t1][:], AF.Exp, [("tmpf", t1)], [("tmpf", t1)], scale=-0.5)
            t2 = base + 1
            tb2 = 2 + i % 2
            if keep:
                dstn = kn[:, THS[th]]
                kkey = [("kn", th)]
                stt(dstn, ps[b_][:], gcol, tmpf[t1][:], ALU.mult, ALU.mult, [("ps", b_), ("tmpf", t1), "vecs"], kkey)
                cpy(tmpb[tb2][:], dstn, kkey, [("tmpb", tb2)], eng="act")
            else:
                stt(tmpb[tb2][:], ps[b_][:], gcol, tmpf[t1][:], ALU.mult, ALU.mult, [("ps", b_), ("tmpf", t1), "vecs"], [("tmpb", tb2)])

        def qkC(i):
            lhs_fn, wkeys, th, dst, dkey, gcol, keep = qk_items[i]
            base = 4 * (i % 2)
            tb2 = 2 + i % 2
            dstn = kn[:, THS[th]] if keep else tmpb[tb2][:]
            kkey = [("kn", th)] if keep else [("tmpb", tb2)]
            b3 = nb()
            mm(ps[b3][:], cb[:, CB_RM:CB_RM + 128], tmpb[tb2][:], True, True, [("tmpb", tb2), "cb"], [("ps", b3)])
            t3, t4 = base + 2, base + 3
            tt(tmpf[t3][:], ps[b3][:], cb[:, CB_RS + th * 512:CB_RS + (th + 1) * 512], ALU.mult, [("ps", b3), "cb"], [("tmpf", t3)])
            tt(tmpf[t4][:], dstn, cb[:, CB_RC + th * 512:CB_RC + (th + 1) * 512], ALU.mult, kkey + ["cb"], [("tmpf", t4)])
            if dst is None:
                tt(KA[0:64, 0, THS[th]], tmpf[t4][0:64, :], tmpf[t3][0:64, :], ALU.add, [("tmpf", t3), ("tmpf", t4)], [dkey[0]])
                tt(KA[64:128, 1, THS[th]], tmpf[t4][64:128, :], tmpf[t3][64:128, :], ALU.add, [("tmpf", t3), ("tmpf", t4)], [dkey[1]])
            else:
                tt(dst, tmpf[t4][:], tmpf[t3][:], ALU.add, [("tmpf", t3), ("tmpf", t4)], dkey)

        NQ = len(qk_items)
        for st_ in range(NQ + 2):
            if st_ < NQ:
                qkA(st_)
            if 0 <= st_ - 1 < NQ:
                qkB(st_ - 1)
            if 0 <= st_ - 2 < NQ:
                qkC(st_ - 2)
        tf_rr[0] = 0
        tb_rr[0] = 0

        if DBG.get('stop') == 'ap2':
            return
        def va_cb(b, blk):
            cpy(stg[0][:, blk * 128:(blk + 1) * 128], ps[b][:, 0:128], [("ps", b)], [("stg0",)], eng="act")
            cpy(VP[:, blk, 0, 0:64], ps[b][:, 0:64], [("ps", b)], [("VP", blk, 0)])
            cpy(VP[:, blk, 1, 64:128], ps[b][:, 64:128], [("ps", b)], [("VP", blk, 1)])
        proj_tm(lambda k: pw1[:, k, 128:256], 128, wk, va_cb)
        if not DBG.get("nonv"):
            dma("sp", nv_d[l].rearrange("(b p) c -> p b c", p=128), stg[0][:].rearrange("p (b c) -> p b c", c=128),
                [("stg0",)], [], ("ystg0",))

        if DBG.get('stop') == 'ap3':
            return
        def kbt_cb(b, blk):
            cpy(kbtok[:, blk, :], ps[b][:, 0:256], [("ps", b)], [("kbtok", blk)], eng="act")
        proj_tm(lambda k: pw1[:, k, 256:512], 256, wk, kbt_cb)
        if DBG.get('stop') == 'ap4':
            return
        for k4 in range(2):
            b = nb()
            for kk in range(4):
                blk = k4 * 4 + kk
                tr(ps[b][:, kk * 128:(kk + 1) * 128], kn[:, blk * 128:(blk + 1) * 128], [("kn", k4)], [("ps", b)])
            cpy(stg[1][:, k4 * 512:(k4 + 1) * 512], ps[b][:], [("ps", b)], [("stg1",)], eng="act")
        dma("sp", nk_d[l].rearrange("(b p) c -> p b c", p=128), stg[1][:].rearrange("p (b c) -> p b c", c=128),
            [("stg1",)], [], ("ystg1",))

        if DBG.get('stop') == 'attnproj':
            return
        for i in range(NB):
            ob_, db_ = (4, 5) if i % 2 == 0 else (6, 7)
            kbs = []
            if i > 0:
                kbs.append(("loc", i - 1, 0 if i % 2 == 0 else 1))
            kbs.append(("loc", i, None))
            if i < NB - 1:
                kbs.append(("loc", i + 1, 2 if i % 2 == 0 else 3))
            kbs.append(("ctx", 0, None))
            kbs.append(("ctx", 1, None))
            steps = [(kind, kb, mk, g) for (kind, kb, mk) in kbs for g in range(2)]
            nmm = len(steps)

            def srcs(n):
                kind, kb, mk, g = steps[n]
                gs = slice(g * 64, (g + 1) * 64)
                if kind == "loc":
                    return (KA[:, g, kb * 128:(kb + 1) * 128], [("KA", kb // 4, g), ("KA", "z%d" % g)], VP[:, kb, g, :],
                            [("VP", kb, g), ("VP", "z%d" % g)], None, gs, mk, g)
                return (KC[:, g, kb * 128:(kb + 1) * 128], [("KC", "v%d" % g), ("KC", "z%d" % g)], VPC[:, kb, g, :],
                        [("VPC", "v%d" % g), ("VPC", "z%d" % g)], V("ctxb"), gs, mk, g)

            def emit_st(n):
                lk, lkey, vsrc, vkey, bias, gs, mk, g = srcs(n)
                sb = n % 4
                mm(ps[sb][:], lk, QA[:, :, i * 128:(i + 1) * 128], True, mk is None,
                   lkey + [("QA", j, i // 4) for j in range(4)], [("ps", sb)])
                if mk is not None:
                    mm(ps[sb][:], identb[:], cb[:, CB_AM + mk * 128:CB_AM + (mk + 1) * 128].unsqueeze(1).to_broadcast([128, 4, 128]),
                       False, True, ["identb", "cb"], [("ps", sb)])

            def emit_rest(n):
                lk, lkey, vsrc, vkey, bias, gs, mk, g = srcs(n)
                sb = n % 4
                tb = ntb()
                if bias is None:
                    act(tmpb[tb][:], ps[sb][:], AF.Exp, [("ps", sb)], [("tmpb", tb)], scale=0.125)
                else:
                    act(tmpb[tb][:], ps[sb][:], AF.Exp, [("ps", sb), "vecs"], [("tmpb", tb)], scale=0.125, bias=bias)
                mm(ps[ob_][:], vsrc, tmpb[tb][:], n == 0, n == nmm - 1, vkey + [("tmpb", tb)], [("ps", ob_)])
                mm(ps[db_][:], ones01[:, g, :], tmpb[tb][:], n == 0, n == nmm - 1, ["ones01", ("tmpb", tb)], [("ps", db_)])

            emit_st(0)
            emit_st(1)
            emit_st(2)
            for n in range(nmm):
                if n + 3 < nmm:
                    emit_st(n + 3)
                emit_rest(n)
            t1 = ntf()
            tt(tmpf[t1][:].rearrange("p (j q) -> p j q", j=4), ps[db_][:].rearrange("p (j q) -> p j q", j=4),
               esink[:, l * 4:(l + 1) * 4].unsqueeze(2).to_broadcast([128, 4, 128]), ALU.add, [("ps", db_), "esink"], [("tmpf", t1)])
            act(tmpf[t1][:], tmpf[t1][:], AF.Ln, [("tmpf", t1)], [("tmpf", t1)])
            act(tmpf[t1][:], tmpf[t1][:], AF.Exp, [("tmpf", t1)], [("tmpf", t1)], scale=-1.0)
            tt(oa[:, :, i * 128:(i + 1) * 128], ps[ob_][:].rearrange("p (j q) -> p j q", j=4),
               tmpf[t1][:].rearrange("p (j q) -> p j q", j=4), ALU.mult, [("ps", ob_), ("tmpf", t1)], [("oa", i)])
            mod_panels(l, 4 + i, 5 + i)
        mod_finish(l, part=2)
        bank_rr[0] = 0

        if DBG.get('stop') == 'attn':
            return
        slot, wk = wload([
            (lambda s, r=r, h=h: s[:, 0:8 * 512].rearrange("p (k h r d) -> p k h r d", k=8, h=4, r=2)[:, :, h, r, :],
             w_in[l, :, 768 + h * 64:768 + (h + 1) * 64].rearrange("(k p) d -> p k d", p=128)) for r in range(2) for h in range(4)])
        pwq = v3(slot, 8, 512)
        for h in range(4):
            proj_fm(lambda k, h=h: pwq[:, k, h * 128:(h + 1) * 128], wk,
                    lambda b, th, h=h: act(QG[:, h, THS[th]], ps[b][:], AF.Identity, [("ps", b)], [("QG", h, th)], scale=0.125))
        slot, wk = wload([
            (lambda s, r=r, h=h: s[:, 0:8 * 560].rearrange("p (k c) -> p k c", c=560)[:, :, h * 128 + r * 64:h * 128 + (r + 1) * 64],
             w_in[l, :, 1024 + h * 64:1024 + (h + 1) * 64].rearrange("(k p) d -> p k d", p=128)) for r in range(2) for h in range(4)] + [
            (lambda s: s[:, 0:8 * 560].rearrange("p (k c) -> p k c", c=560)[:, :, 512:528], wsrc(w_in, l, 2304, 16)),
            (lambda s: s[:, 0:8 * 560].rearrange("p (k c) -> p k c", c=560)[:, :, 528:544], wsrc(w_in, l, 2304, 16)),
            (lambda s: s[:, 0:8 * 560].rearrange("p (k c) -> p k c", c=560)[:, :, 544:560], wsrc(w_in, l, 2320, 16))])
        pwk = slot[:, 0:8 * 560].rearrange("p (k c) -> p k c", c=560)
        for h in range(4):
            proj_fm(lambda k, h=h: pwk[:, k, h * 128:(h + 1) * 128], wk,
                    lambda b, th, h=h: cpy(KG[:, h, THS[th]], ps[b][:], [("ps", b)], [("KG", h, th)]))

        def gl_cb(b, th):
            cpy(aug2[0:16, THS[th]], ps[b][0:16, :], [("ps", b)], [("aug2", th, 0)], eng="act")
            cpy(aug2[32:48, THS[th]], ps[b][32:48, :], [("ps", b)], [("aug2", th, 1)])
        proj_fm(lambda k: pwk[:, k, 512:560], wk, gl_cb)
        slot, wk_vb = wload([(lambda s: v3(s, 8, 512), wsrc(w_in, l, 1280, 512))])
        pwv = v3(slot, 8, 512)

        def vb_block(blk):
            b_ = nb()
            for k in range(8):
                mm(ps[b_][:, 0:512], hn[:, k, blk * 128:(blk + 1) * 128], pwv[:, k, :], k == 0, k == 7,
                   list(wk_vb) + [("hn", k, blk // 4)], [("ps", b_)])
            cpy(Vg[:, blk, :], ps[b_][:], [("ps", b_)], [("Vg", blk)])

        if DBG.get('stop') == 'glaproj':
            return
        TTm = cf[:, CF_TT:CF_TT + 256]

        def preA(blk):
            th = blk // 4
            bs = slice(blk * 128, (blk + 1) * 128)
            base = 4 * (blk % 2)
            b_ = nb()
            mm(ps[b_][:], aug2[:, bs], w2all[:, :], True, True,
               [("aug2", th, 0), ("aug2", th, 1), "aug2", "w2all"], [("ps", b_)])
            te, tl = base, base + 1
            act(tmpf[te][:], ps[b_][:], AF.Exp, [("ps", b_)], [("tmpf", te)], scale=-1.0)
            act(tmpf[tl][:], tmpf[te][:], AF.Ln, [("tmpf", te)], [("tmpf", tl)], bias=1.0, scale=1.0)

        def preB(blk):
            th = blk // 4
            bs = slice(blk * 128, (blk + 1) * 128)
            base = 4 * (blk % 2)
            tl, td, tq, tk = base + 1, base + 2, base + 3, base
            lt4 = tmpf[tl][:].rearrange("p (h r d) -> p h r d", h=4, r=2)
            b_ = nb()
            mm(ps[b_][:, 0:256], cf[:, CF_SU:CF_SU + 128], lt4[:, :, 0, :], True, True, [("tmpf", tl), "cf"], [("ps", b_)])
            mm(ps[b_][:, 256:512], cf[:, CF_SL:CF_SL + 128], lt4[:, :, 1, :], True, True, [("tmpf", tl), "cf"], [("ps", b_)])
            bb = [nb(), nb()]
            for h in range(4):
                mm(ps[bb[h // 2]][:, (h % 2) * 256:(h % 2 + 1) * 256], tmpf[tl][:, h * 128:(h + 1) * 128], TTm, True, True,
                   [("tmpf", tl), "cf"], [("ps", bb[h // 2])])
            act(tmpf[td][:], ps[b_][:], AF.Exp, [("ps", b_)], [("tmpf", td)], scale=-1.0 / 16)
            for r in range(2):
                tt(Kp[:, blk, :, r, :], kbtok[:, blk, :].rearrange("p (h d) -> p h d", h=4),
                   tmpf[td][:, r * 256:(r + 1) * 256].rearrange("p (h d) -> p h d", h=4), ALU.mult,
                   [("kbtok", blk), ("tmpf", td)], [("Kp", blk, r)])
            Eq = tmpf[tq][:].rearrange("p (h t) -> p h t", h=4)
            Ek = tmpf[tk][:].rearrange("p (h t) -> p h t", h=4)
            for hb in range(2):
                pin = ps[bb[hb]][:].rearrange("p (h c) -> p h c", h=2)
                hs = slice(hb * 2, hb * 2 + 2)
                act(Eq[0:64, hs, :], pin[0:64, :, 0:128], AF.Exp, [("ps", bb[hb])], [("tmpf", tq)], scale=-1.0 / 16)
                act(Eq[64:128, hs, :], pin[64:128, :, 128:256], AF.Exp, [("ps", bb[hb])], [("tmpf", tq)], scale=-1.0 / 16)
                act(Ek[0:64, hs, :], pin[0:64, :, 0:128], AF.Exp, [("ps", bb[hb])], [("tmpf", tk)], scale=1.0 / 16)
                act(Ek[64:128, hs, :], pin[64:128, :, 128:256], AF.Exp, [("ps", bb[hb])], [("tmpf", tk)], scale=1.0 / 16)
            tt(QG[:, :, bs], QG[:, :, bs], Eq, ALU.mult, [("QG", h, th) for h in range(4)] + [("tmpf", tq)], [("QG", h, th) for h in range(4)])
            tt(KG[:, :, bs], KG[:, :, bs], Ek, ALU.mult, [("KG", h, th) for h in range(4)] + [("tmpf", tk)], [("KG", h, th) for h in range(4)])
            for cc in range(2):
                c = 2 * blk + cc
                cpy(Gt[0:64, :, c:c + 1], Eq[0:64, :, cc * 64 + 63:cc * 64 + 64], [("tmpf", tq)], ["Gt"])
                cpy(Gt[64:128, :, 15 - c:16 - c], Eq[64:128, :, cc * 64:cc * 64 + 1], [("tmpf", tq)], ["Gt"])

        for st_ in range(NB + 1):
            if st_ < NB:
                preA(st_)
            if st_ >= 1:
                preB(st_ - 1)
            if st_ < NB:
                vb_block(st_)
        tf_rr[0] = 0

        if DBG.get('stop') == 'glapre':
            return
        dma("sp", Sst[0:64, :, :], s0_d[l, 0].rearrange("h d e -> d h e"), [], [("Sst", 0)], ("Sst", 0))
        dma("sp", Sst[64:128, :, :], s0_d[l, 1].rearrange("h d e -> d h e"), [], [("Sst", 1)], ("Sst", 1))
        slot, wk_rb = wload([(lambda s: v3(s, 8, 512), wsrc(w_in, l, 1792, 512))])
        pwr = v3(slot, 8, 512)

        def rb_group(gi):
            h, th = gi // 2, gi % 2
            b_ = nb()
            for k in range(8):
                mm(ps[b_][:], pwr[:, k, h * 128:(h + 1) * 128], hn[:, k, THS[th]], k == 0, k == 7, wk_rb + [("hn", k, th)], [("ps", b_)])
            act(ob[:, h, THS[th]], ps[b_][:], AF.Silu, [("ps", b_)], [("ob", "rb", h, th)])

        for n in range(16):
            cfw, cbw = n, 15 - n
            if n % 2 == 1:
                rb_group(n // 2)
            cpy(S_all[:, n, :, :], Sst[:, :, :], [("Sst", 0), ("Sst", 1)], [("S_all", n)], eng="dve")
            kvt = n % 4
            for r, c in ((0, cfw), (1, cbw)):
                b = nb()
                prt = slice((c % 2) * 64, (c % 2) * 64 + 64)
                for h in range(4):
                    mm(ps[b][:, h * 128:(h + 1) * 128], Kp[prt, c // 2, h, :, :].rearrange("p r d -> p (r d)"),
                       Vg[prt, c // 2, h * 128:(h + 1) * 128], True, True,
                       [("Kp", c // 2, 0), ("Kp", c // 2, 1), ("Vg", c // 2)], [("ps", b)])
                rs = slice(r * 64, (r + 1) * 64)
                act(tmpf[kvt][rs, :], ps[b][rs, :], AF.Copy, [("ps", b)], [("tmpf", kvt)])
            tt(Sst[:, :, :], Sst[:, :, :], Gt[:, :, n:n + 1].to_broadcast([128, 4, 128]), ALU.mult,
               [("Sst", 0), ("Sst", 1), "Gt"], [("Sst", 0), ("Sst", 1)])
            tt(Sst[:, :, :], Sst[:, :, :], tmpf[kvt][:].rearrange("p (h e) -> p h e", h=4), ALU.add,
               [("Sst", 0), ("Sst", 1), ("tmpf", kvt)], [("Sst", 0), ("Sst", 1)])
            if (n + 1) % 4 == 0:
                sqf, sqb = cfw // 4, cbw // 4
                cpy(stSs[sqf // 2][0:64, sqf % 2, :, :], Sst[0:64, :, :], [("Sst", 0)], [("stS" + "ab"[sqf // 2], 0, sqf % 2)], eng="act")
                cpy(stSs[sqb // 2][64:128, sqb % 2, :, :], Sst[64:128, :, :], [("Sst", 1)], [("stS" + "ab"[sqb // 2], 1, sqb % 2)], eng="act")
                if n < 15:
                    ts(Sst[:].rearrange("p h e -> p (h e)"), Sst[:].rearrange("p h e -> p (h e)"), V("mres"), None, ALU.mult, None,
                       [("Sst", 0), ("Sst", 1), "vecs"], [("Sst", 0), ("Sst", 1)])
        for r in range(2):
            for sq in range(4):
                dma("sp", nst_d[l, sq, r].rearrange("h d e -> d h e"), stSs[sq // 2][r * 64:(r + 1) * 64, sq % 2, :, :],
                    [("stS" + "ab"[sq // 2], r, sq % 2)], [], ("stS", r, sq))

        if DBG.get('stop') == 'scan':
            return
        if DBG.get('stop') == 'rb':
            return
        maskA = cb[:, CB_MA:CB_MA + 256].rearrange("p (r t) -> p r t", r=2)
        gst = {}

        def goA(blk):
            th = blk // 4
            bs = slice(blk * 128, (blk + 1) * 128)
            for r in range(2):
                rs = slice(r * 64, (r + 1) * 64)
                tat = 2 * (blk % 2) + r
                b_ = nb()
                pa = ps[b_][:].rearrange("p (h t) -> p h t", h=4)
                for h in range(4):
                    mm(pa[:, h, :], KG[rs, h, bs], QG[rs, h, bs], True, True, [("KG", h, th), ("QG", h, th)], [("ps", b_)])
                tt(tmpb[tat][:].rearrange("p (h t) -> p h t", h=4), pa, maskA[:, r, :].unsqueeze(1).to_broadcast([128, 4, 128]),
                   ALU.mult, [("ps", b_), "cb"], [("tmpb", tat)])

        def goB(blk):
            th = blk // 4
            base = 4 * (blk % 2)
            bo = [nb(), nb()]
            for r in range(2):
                rs = slice(r * 64, (r + 1) * 64)
                tat = 2 * (blk % 2) + r
                po = ps[bo[r]][:].rearrange("p (h t) -> p h t", h=4)
                at3 = tmpb[tat][:].rearrange("p (h t) -> p h t", h=4)
                for h in range(4):
                    mm(po[:, h, :], Vg[:, blk, h * 128:(h + 1) * 128], at3[:, h, :], True, False, [("Vg", blk), ("tmpb", tat)], [("ps", bo[r])])
                    for cc in range(2):
                        c = 2 * blk + cc
                        tsl = slice(blk * 128 + cc * 64, blk * 128 + cc * 64 + 64)
                        si = c if r == 0 else 15 - c
                        mm(po[:, h, cc * 64:(cc + 1) * 64], S_all[rs, si, h, :], QG[rs, h, tsl], False, cc == 1,
                           [("S_all", si), ("QG", h, th)], [("ps", bo[r])])
            tc_, to_ = base, base + 1
            act(tmpf[tc_][:], ps[bo[1]][:], AF.Copy, [("ps", bo[1])], [("tmpf", tc_)])
            tt(tmpf[to_][:], ps[bo[0]][:], tmpf[tc_][:], ALU.add, [("ps", bo[0]), ("tmpf", tc_)], [("tmpf", to_)])
            tb = 4 + blk % 2
            act(tmpb[tb][:], tmpf[to_][:], AF.Square, [("tmpf", to_)], [("tmpb", tb)])

        def goC(blk):
            th = blk // 4
            bs = slice(blk * 128, (blk + 1) * 128)
            base = 4 * (blk % 2)
            to_, t1, t2 = base + 1, base + 2, base + 3
            tb = 4 + blk % 2
            b2 = nb()
            mm(ps[b2][:], onesb[:], tmpb[tb][:], True, True, [("tmpb", tb), "onesb"], [("ps", b2)])
            act(tmpf[t1][:], ps[b2][:], AF.Ln, [("ps", b2)], [("tmpf", t1)], bias=EPS, scale=1.0 / 128)
            act(tmpf[t1][:], tmpf[t1][:], AF.Exp, [("tmpf", t1)], [("tmpf", t1)], scale=-0.5)
            stt(tmpf[t2][:], tmpf[to_][:], V("ggla", idx=l), tmpf[t1][:], ALU.mult, ALU.mult, [("tmpf", to_), ("tmpf", t1), "vecs"], [("tmpf", t2)])
            tt(ob[:, :, bs], tmpf[t2][:].rearrange("p (h t) -> p h t", h=4), ob[:, :, bs], ALU.mult,
               [("tmpf", t2)] + [("ob", "rb", h, th) for h in range(4)], [("ob", blk)])

        slot, wk_uc = wload([(lambda s: v3(s, 8, 512), wsrc(w_in, l, 2336, 512))])
        pwu = v3(slot, 8, 512)

        def uc_block(blk):
            b_ = nb()
            for k in range(8):
                mm(ps[b_][:, 0:512], hn[:, k, blk * 128:(blk + 1) * 128], pwu[:, k, :], k == 0, k == 7,
                   list(wk_uc) + [("hn", k, blk // 4)], [("ps", b_)])
            cpy(uctok[:, blk, :], ps[b_][:], [("ps", b_)], [("uctok", blk)], eng=("act" if blk % 2 == 0 else "dve"))

        for st_ in range(NB + 2):
            if st_ < NB:
                goA(st_)
            if 0 <= st_ - 1 < NB:
                goB(st_ - 1)
            if 0 <= st_ - 2 < NB:
                goC(st_ - 2)
            if st_ < NB:
                uc_block(st_)
        tf_rr[0] = 0
        tb_rr[0] = 0

        if DBG.get('stop') == 'glaout':
            return
        def band(g, kind):
            o = CB_BAND + (g * 8 + kind) * 128
            return cb[:, o:o + 128]
        for g in range(4):
            for th in range(2):
                b = nb()
                for ii in range(4):
                    i = th * 4 + ii
                    terms = []
                    if i > 0:
                        terms.append((i - 1, 4 if i % 2 == 1 else 5))
                    terms.append((i, 0 if i == 0 else (1 if i == 7 else (2 if i % 2 == 0 else 3))))
                    if i < 7:
                        terms.append((i + 1, 6 if i % 2 == 0 else 7))
                    for ti, (j, kind) in enumerate(terms):
                        mm(ps[b][:, ii * 128:(ii + 1) * 128], uctok[:, j, g * 128:(g + 1) * 128], band(g, kind),
                           ti == 0, ti == len(terms) - 1, [("uctok", j), "cb"], [("ps", b)])
                tb = ntb()
                cpy(tmpb[tb][:], ps[b][:], [("ps", b)], [("tmpb", tb)], eng="act")
                b2 = nb()
                mm(ps[b2][:], wpool[:, g, :], tmpb[tb][:], True, True, ["wpool", ("tmpb", tb)], [("ps", b2)])
                act(oc[:, g, THS[th]], ps[b2][:], AF.Identity, [("ps", b2), "vecs"], [("oc", g, th)], scale=V("psc", idx=l * 4 + g))

        if DBG.get('stop') == 'pool':
            return
        obr_keys = [lambda kk, th: [("oa", th * 4 + q) for q in range(4)],
                    lambda kk, th: [("ob", th * 4 + q) for q in range(4)],
                    lambda kk, th: [("oc", kk, th)]]
        obr_src = [lambda kk, th: oa[:, kk, THS[th]], lambda kk, th: ob[:, kk, THS[th]], lambda kk, th: oc[:, kk, THS[th]]]
        for m in range(8):
            mc = slice(m * 128, (m + 1) * 128)
            slotg, wkg = wload([(lambda s, br=br: v3(s, 8, 384)[:, :, br * 128:(br + 1) * 128],
                                 wsrc(w_in, l, 2848 + br * 1024 + m * 128, 128)) for br in range(3)])
            pg = v3(slotg, 8, 384)
            slotb, wkb = wload([
                (lambda s: v3(s, 4, 384)[0:64, :, 0:128], w_br[0][l, 0:256, mc].rearrange("(j d) c -> d j c", d=64)),
                (lambda s: v3(s, 4, 384)[64:128, :, 0:128], w_br[0][l, 256:512, mc].rearrange("(j d) c -> d j c", d=64)),
                (lambda s: v3(s, 4, 384)[:, :, 128:256], w_br[1][l, :, mc].rearrange("(k p) c -> p k c", p=128)),
                (lambda s: v3(s, 4, 384)[:, :, 256:384], w_br[2][l, :, mc].rearrange("(k p) c -> p k c", p=128))])
            pb = v3(slotb, 4, 384)
            for th in range(2):
                tacc = ntf()
                for br in range(3):
                    bg = nb()
                    for k in range(8):
                        mm(ps[bg][:], pg[:, k, br * 128:(br + 1) * 128], hn[:, k, THS[th]], k == 0, k == 7, wkg + [("hn", k, th)], [("ps", bg)])
                    bbk = nb()
                    for kk in range(4):
                        mm(ps[bbk][:], pb[:, kk, br * 128:(br + 1) * 128], obr_src[br](kk, th), kk == 0, kk == 3,
                           wkb + obr_keys[br](kk, th), [("ps", bbk)])
                    tsg = ntf()
                    while tsg == tacc:
                        tsg = ntf()
                    act(tmpf[tsg][:], ps[bg][:], AF.Sigmoid, [("ps", bg)], [("tmpf", tsg)])
                    if br == 0:
                        tt(tmpf[tacc][:], tmpf[tsg][:], ps[bbk][:], ALU.mult, [("tmpf", tsg), ("ps", bbk)], [("tmpf", tacc)])
                    else:
                        tt(tmpf[tsg][:], tmpf[tsg][:], ps[bbk][:], ALU.mult, [("tmpf", tsg), ("ps", bbk)], [("tmpf", tsg)])
                        if br == 1:
                            tt(tmpf[tacc][:], tmpf[tacc][:], tmpf[tsg][:], ALU.add, [("tmpf", tsg), ("tmpf", tacc)], [("tmpf", tacc)])
                        else:
                            tt(mixed[:, m, THS[th]], tmpf[tacc][:], tmpf[tsg][:], ALU.add, [("tmpf", tsg), ("tmpf", tacc)], [("mixed", m, th)])

        if DBG.get('stop') == 'merge':
            return
        for half in range(2):
            slot, wk = wload([(lambda s: v3(s, 8, 512), wsrc(w_out, l, half * 512, 512))])
            pwo = v3(slot, 8, 512)
            for mm_ in range(4):
                m = half * 4 + mm_
                for th in range(2):
                    b = nb()
                    for k in range(8):
                        mm(ps[b][:], pwo[:, k, mm_ * 128:(mm_ + 1) * 128], mixed[:, k, THS[th]], k == 0, k == 7,
                           wk + [("mixed", k, th)], [("ps", b)])
                    stt(xT[:, m, THS[th]], ps[b][:], modT[:, 16 + m:17 + m], xT[:, m, THS[th]], ALU.mult, ALU.add,
                        [("ps", b), mk_, ("xT", m, th)], [("xT", m, th)])

        if DBG.get('stop') == 'outproj':
            return
        norm_mod(lambda k: A12[:, 8 + k:9 + k], lambda k: modT[:, 24 + k:25 + k], (ak_, 1), mk_,
                 filler=(lambda: mod_panels(l + 1, 0, 3)) if l + 1 < NL else None)
        for fc in range(11):
            slg, wkg = wload([(lambda s: v3(s, 8, 512)[:, :, 0:256], wsrc(w_fg, l, fc * 256, 256)),
                              (lambda s: v3(s, 8, 512)[:, :, 256:512], wsrc(w_fu, l, fc * 256, 256))])
            pgu = v3(slg, 8, 512)
            for ff in range(2):
                f = fc * 2 + ff
                for th in range(2):
                    bg = nb()
                    for k in range(8):
                        mm(ps[bg][:], pgu[:, k, ff * 128:(ff + 1) * 128], hn[:, k, THS[th]], k == 0, k == 7, wkg + [("hn", k, th)], [("ps", bg)])
                    bu = nb()
                    for k in range(8):
                        mm(ps[bu][:], pgu[:, k, 256 + ff * 128:256 + (ff + 1) * 128], hn[:, k, THS[th]], k == 0, k == 7, wkg + [("hn", k, th)], [("ps", bu)])
                    tsg = ntf()
                    act(tmpf[tsg][:], ps[bg][:], AF.Silu, [("ps", bg)], [("tmpf", tsg)])
                    tt(hT[:, f, THS[th]], tmpf[tsg][:], ps[bu][:], ALU.mult, [("tmpf", tsg), ("ps", bu)], [("hT", f, th)])
            if l + 1 < NL and fc == 0:
                mod_panels(l + 1, 3, 4)
        if l + 1 < NL:
            mod_finish(l + 1, part=1)
        for m in range(8):
            slot, wk = wload([(lambda s: v3(s, NFT, 128), w_fd[l, :, m * 128:(m + 1) * 128].rearrange("(k p) c -> p k c", p=128))])
            pd = v3(slot, NFT, 128)
            for th in range(2):
                b = nb()
                for f in range(NFT):
                    mm(ps[b][:], pd[:, f, :], hT[:, f, THS[th]], f == 0, f == NFT - 1, wk + [("hT", f, th)], [("ps", b)])
                stt(xT[:, m, THS[th]], ps[b][:], modT[:, 40 + m:41 + m], xT[:, m, THS[th]], ALU.mult, ALU.add,
                    [("ps", b), mk_, ("xT", m, th)], [("xT", m, th)])
            if l == NL - 1 and not DBG.get('skipy') and m >= 1:
                y_tile(m - 1)
        if l == NL - 1 and not DBG.get('skipy'):
            y_tile(7)

    def y_tile(m):
        st = stg[m % 4]
        sk = ("stg%d" % (m % 4),)
        for b4 in range(2):
            b = nb()
            for kk in range(4):
                blk = b4 * 4 + kk
                tr(ps[b][:, kk * 128:(kk + 1) * 128], xT[:, m, blk * 128:(blk + 1) * 128], [("xT", m, blk // 4)], [("ps", b)])
            cpy(st[:, b4 * 512:(b4 + 1) * 512], ps[b][:], [("ps", b)], [sk], eng=("act" if b4 == 0 else "dve"))
        dma("sp", y_d[:, m * 128:(m + 1) * 128].rearrange("(b p) c -> p b c", p=128), st[:].rearrange("p (b c) -> p b c", c=128),
            [sk], [], ("ystg%d" % (m % 4),))

    NL = DBG.get('L', L)
    if NL > 0:
        mod_panels(0, 0, 4)
        mod_finish(0, part=1)
    dma("pool", cb[:], cb_d, [], ["cb"], "cb")
    for l in range(NL):
        layer(l)

    tens = {"hn": hn, "QA": QA, "KA": KA, "kn": kn, "oa": oa, "QG": QG, "KG": KG, "Kp": Kp, "Vg": Vg, "ob": ob, "oc": oc,
            "mixed": mixed, "xT": xT, "modT": modTs[0], "S_all": S_all, "VP": VP, "KC": KC, "Gt": Gt, "aug2": aug2, "uctok": uctok}
    for (dname, dkeys) in DBG.get('dump', []):
        th_ = tens[dname]
        dd = nc.dram_tensor('dbg_' + dname, [int(v) for v in th_.shape], F32, kind='ExternalOutput').ap()
        dma('pool', dd, th_[:], ['*'], [], ('dbg', dname))

    for blk in range(0 if (DBG.get('skipy') or NL > 0) else NB):
        st = stg[blk % 4]
        for k4 in range(2):
            b = nb()
            for kk in range(4):
                k = k4 * 4 + kk
                tr(ps[b][:, kk * 128:(kk + 1) * 128], xT[:, k, blk * 128:(blk + 1) * 128], [("xT", k, blk // 4)], [("ps", b)])
            cpy(st[:, k4 * 512:(k4 + 1) * 512], ps[b][:], [("ps", b)], [("stg%d" % (blk % 4),)], eng=("act" if k4 == 0 else "dve"))
        dma("sp", y_d[blk * 128:(blk + 1) * 128, :], st[:], [("stg%d" % (blk % 4),)], [], ("ystg%d" % (blk % 4),))

    P.emit()
    return nc, total_sbuf


def _consts(role):
    s = np.arange(128)[:, None]
    t = np.arange(128)[None, :]
    same = (s // 64) == (t // 64)
    cfm = np.zeros((128, NCF), np.float32)
    cfm[:, CF_TT:CF_TT + 128] = same & (s <= t)
    cfm[:, CF_TT + 128:CF_TT + 256] = same & (s >= t)
    cfm[:, CF_SU:CF_SU + 128] = same & (s > t)
    cfm[:, CF_SL:CF_SL + 128] = same & (s < t)
    p = np.arange(128)
    dd = p % 64
    tok = np.arange(T)
    if role == "sample":
        i = dd % 16
        inv = (10000.0 ** (-(i.astype(np.float32)) / np.float32(16.0))).astype(np.float32)
        pos = np.where((dd < 32)[:, None], (tok // 64)[None, :], (tok % 64)[None, :]).astype(np.float32)
        ang = (pos * inv[:, None]).astype(np.float32)
        sgn = np.where((dd % 32) < 16, -1.0, 1.0)[:, None]
        ropec, ropes = np.cos(ang), np.sin(ang) * sgn
    else:
        ropec, ropes = np.ones((128, T), np.float32), np.zeros((128, T), np.float32)
    cbm = np.zeros((128, NCB), np.float32)
    cbm[:, CB_RC:CB_RC + T] = ropec
    cbm[:, CB_RS:CB_RS + T] = ropes
    partner = np.where((dd % 32) < 16, p + 16, p - 16)
    cbm[partner, CB_RM + p] = 1.0
    cbm[:, CB_MA:CB_MA + 128] = same & (s <= t)
    cbm[:, CB_MA + 128:CB_MA + 256] = same & (s >= t)
    NEGM = -30000.0
    if role == "sample":
        prev = np.where(s >= t, 0.0, NEGM)
        nxt = np.where(s <= t, 0.0, NEGM)
        am = [prev, prev, nxt, nxt]
    else:
        z = np.zeros((128, 128))
        n_ = np.full((128, 128), NEGM)
        am = [n_, z, z, n_]
    for k in range(4):
        cbm[:, CB_AM + k * 128:CB_AM + (k + 1) * 128] = am[k]
    Ls = 1024 if role == "sample" else 256
    for g, w in enumerate((2, 4, 8, 16)):
        Wf = np.zeros((T, T), np.float64)
        for tt_ in range(T):
            s0_ = (tt_ // Ls) * Ls
            tl = tt_ - s0_
            lo = min(max(tl - w // 2, 0), Ls)
            hi = min(max(tl - w // 2 + w, 0), Ls)
            Wf[s0_ + lo:s0_ + hi, tt_] = 1.0 / (hi - lo)
            Wf[tt_, tt_] -= 1.0
        blkm = lambda j, i: Wf[j * 128:(j + 1) * 128, i * 128:(i + 1) * 128]
        kinds = [blkm(0, 0), blkm(7, 7), blkm(2, 2), blkm(1, 1), blkm(0, 1), blkm(1, 2), blkm(1, 0), blkm(2, 1)]
        for i in (2, 4, 6):
            assert np.array_equal(blkm(i, i), kinds[2]) and np.array_equal(blkm(i - 1, i), kinds[5]) and np.array_equal(blkm(i + 1, i), kinds[6])
        for i in (1, 3, 5):
            assert np.array_equal(blkm(i, i), kinds[3]) and np.array_equal(blkm(i - 1, i), kinds[4]) and np.array_equal(blkm(i + 1, i), kinds[7])
        assert np.array_equal(blkm(1, 0), kinds[6]) and np.array_equal(blkm(6, 7), kinds[4])
        for kind in range(8):
            o = CB_BAND + (g * 8 + kind) * 128
            cbm[:, o:o + 128] = kinds[kind]
    return cfm, cbm


def kernel(x_prompt, x_sample, c, cache_k, cache_v, state_gla, c_ctx, w_in, g_qn, g_kn, att_sink,
           w_gate2, b_gate2, g_gla_out, w_pool, pool_scale, w_br_a, w_br_b, w_br_c, w_out,
           g_norm1, g_norm2, w_mod, b_mod, w_ff_gate, w_ff_up, w_ff_down):
    f = lambda a: np.ascontiguousarray(np.asarray(a, dtype=np.float32))
    x_prompt, x_sample, c, cache_k, cache_v, state_gla, c_ctx = map(f, (x_prompt, x_sample, c, cache_k, cache_v, state_gla, c_ctx))
    g_qn, g_kn, att_sink, w_gate2, b_gate2, g_gla_out, pool_scale = map(f, (g_qn, g_kn, att_sink, w_gate2, b_gate2, g_gla_out, pool_scale))
    g_norm1, g_norm2, b_mod = map(f, (g_norm1, g_norm2, b_mod))
    shared = {"w_in": f(w_in), "w_pool": f(w_pool), "w_br_a": f(w_br_a), "w_br_b": f(w_br_b), "w_br_c": f(w_br_c),
              "w_out": f(w_out), "w_mod": f(w_mod), "w_ff_gate": f(w_ff_gate), "w_ff_up": f(w_ff_up), "w_ff_down": f(w_ff_down)}
    w2 = np.zeros((64, L, 4, 2, 64), np.float32)
    for l in range(L):
        for r in range(2):
            w2[r * 32:r * 32 + 16, l, :, r, :] = w_gate2[l, r].reshape(16, 4, 64)
            w2[r * 32 + 16, l, :, r, :] = b_gate2[l, r].reshape(4, 64)
    shared["w2aug"] = np.ascontiguousarray(w2.reshape(64, L * 512))
    p = np.arange(128)

    def vec_base(cv, ctxb, mres):
        v = np.zeros((128, NVEC), np.float32)
        v[:, VEC_OFF["gn1"]:VEC_OFF["gn1"] + 32] = g_norm1.reshape(L, 8, 128).transpose(2, 0, 1).reshape(128, 32)
        v[:, VEC_OFF["gn2"]:VEC_OFF["gn2"] + 32] = g_norm2.reshape(L, 8, 128).transpose(2, 0, 1).reshape(128, 32)
        v[:, VEC_OFF["gq"]:VEC_OFF["gq"] + 4] = g_qn[:, p % 64].T
        v[:, VEC_OFF["gk"]:VEC_OFF["gk"] + 4] = g_kn[:, p % 64].T
        v[:, VEC_OFF["ggla"]:VEC_OFF["ggla"] + 4] = g_gla_out.T
        v[:, VEC_OFF["psc"]:VEC_OFF["psc"] + 16] = pool_scale.reshape(L, 4, 128).transpose(2, 0, 1).reshape(128, 16)
        sk = att_sink.reshape(L, 2, 4)
        v[:, VEC_OFF["sink"]:VEC_OFF["sink"] + 16] = sk[:, p // 64, :].transpose(1, 0, 2).reshape(128, 16)
        v[:, VEC_OFF["cvec"]:VEC_OFF["cvec"] + 8] = cv.reshape(8, 128).T
        v[:, VEC_OFF["ctxb"]] = ctxb
        v[:, VEC_OFF["mres"]] = mres
        v[:, VEC_OFF["bmod"]:VEC_OFF["bmod"] + 192] = b_mod.reshape(L, 48, 128).transpose(2, 0, 1).reshape(128, 192)
        return v

    cons = {"prompt": _consts("prompt"), "sample": _consts("sample")}
    in_maps = []
    for core in range(8):
        m = dict(shared)
        if core < 4:
            m["x_in"] = np.ascontiguousarray(x_prompt[core * 4:(core + 1) * 4].reshape(T, D))
            m["vecs"] = vec_base(c_ctx, -1e30, 0.0)
            m["cf"], m["cb"] = cons["prompt"]
            m["ctxk"] = np.zeros((L, 256, 128), np.float32)
            m["ctxv"] = np.zeros((L, 256, 128), np.float32)
            m["s0"] = np.zeros((L, 2, 4, 64, 128), np.float32)
        else:
            bi = (core - 4) % 2
            m["x_in"] = np.ascontiguousarray(x_sample[bi])
            m["vecs"] = vec_base(c[bi], 0.0, 1.0)
            m["cf"], m["cb"] = cons["sample"]
            m["ctxk"] = np.ascontiguousarray(cache_k[bi].reshape(L, 256, 128))
            m["ctxv"] = np.ascontiguousarray(cache_v[bi].reshape(L, 256, 128))
            m["s0"] = np.ascontiguousarray(state_gla[bi])
        in_maps.append(m)
    nc, _ = build_program()
    res = run_bass_kernel_spmd(nc, in_maps, core_ids=list(range(8)))
    r = res.results
    y_prompt = np.concatenate([r[cidx]["y"].reshape(4, 256, D) for cidx in range(4)], axis=0)
    y_sample = np.stack([r[4]["y"], r[5]["y"]], axis=0)
    nk = np.concatenate([r[cidx]["nk"].reshape(L, 4, 256, 2, 64).transpose(1, 0, 2, 3, 4) for cidx in range(4)], axis=0)
    nv = np.concatenate([r[cidx]["nv"].reshape(L, 4, 256, 2, 64).transpose(1, 0, 2, 3, 4) for cidx in range(4)], axis=0)
    nst = np.concatenate([r[cidx]["nst"].transpose(1, 0, 2, 3, 4, 5) for cidx in range(4)], axis=0)
    return (np.ascontiguousarray(y_prompt, dtype=np.float32), np.ascontiguousarray(y_sample, dtype=np.float32),
            np.ascontiguousarray(nk, dtype=np.float32), np.ascontiguousarray(nv, dtype=np.float32),
            np.ascontiguousarray(nst, dtype=np.float32))
```
